# Optimizing a Trainium2 kernel written in Bass

```python
import math
import jax, jax.numpy as jnp
from jax import lax
import numpy as np

D_MODEL = 2048
BATCH = 2
SEQ = 8192
DEPTH = 1
DEC_BATCH = 128
DEC_SEQ = 4
PAST_LEN = 16384
PAGE_SIZE = 128

D_MIX = D_MODEL
D_RNN = D_MIX // 2
D_ATTN = D_MIX - D_RNN
HEAD_DIM = 64
N_HEADS = D_ATTN // HEAD_DIM
KV_HEADS = 4
GQA_GROUP = N_HEADS // KV_HEADS
KV_DIM = KV_HEADS * HEAD_DIM
RNN_BLOCKS = 16
RNN_BLOCK = D_RNN // RNN_BLOCKS
CONV_W = 4
RG_C = 8.0
WINDOW = 128
BLOCK = WINDOW
CACHE_WIN = min(WINDOW, PAST_LEN)
N_BUCKETS = 32
MAX_EXACT = N_BUCKETS // 2
REL_MAX_DIST = 128
D_FF = 3 * D_MODEL
FFN_CONV_W = 3
EPS = 1e-6
NEG_INF = -1e30
ATTN_SCALE = HEAD_DIM ** -0.5
D_IN_PROJ = 2 * D_RNN + D_ATTN + 2 * KV_DIM
SPLIT_IDX = (D_RNN, 2 * D_RNN, 2 * D_RNN + D_ATTN, 2 * D_RNN + D_ATTN + KV_DIM)

kernel_name = "hymba_rglru_swa_sink_convffn_step"


def _rmsnorm(x, g):
    xf = x.astype(jnp.float32)
    y = xf * lax.rsqrt(jnp.mean(xf * xf, axis=-1, keepdims=True) + EPS)
    return (y * g.astype(jnp.float32)).astype(x.dtype)


def _causal_dwconv(x, buf, w, b):
    K = w.shape[0]
    T = x.shape[1]
    xp = jnp.concatenate([buf.astype(x.dtype), x], axis=1)
    y = b + sum(xp[:, k:k + T] * w[k] for k in range(K))
    return y, xp[:, xp.shape[1] - (K - 1):]


def _rglru(x, h0, w_a, b_a, w_x, b_x, lam):
    B, T, C = x.shape
    xb = x.reshape(B, T, RNN_BLOCKS, RNN_BLOCK)
    r = jax.nn.sigmoid((jnp.einsum("bthi,hij->bthj", xb, w_a).reshape(B, T, C) + b_a).astype(jnp.float32))
    i = jax.nn.sigmoid((jnp.einsum("bthi,hij->bthj", xb, w_x).reshape(B, T, C) + b_x).astype(jnp.float32))
    log_a = -RG_C * r * jax.nn.softplus(-lam.astype(jnp.float32))
    a = jnp.exp(log_a)
    u = jnp.sqrt(-jnp.expm1(2.0 * log_a)) * (i * x.astype(jnp.float32))

    def step(h, au):
        a_t, u_t = au
        h = a_t * h + u_t
        return h, h

    hT, hs = lax.scan(step, h0.astype(jnp.float32), (jnp.swapaxes(a, 0, 1), jnp.swapaxes(u, 0, 1)))
    return jnp.swapaxes(hs, 0, 1).astype(x.dtype), hT.astype(x.dtype)


def _t5_bucket(d):
    n = jnp.maximum(d, 0)
    nf = jnp.maximum(n, 1).astype(jnp.float32)
    large = MAX_EXACT + (jnp.log(nf / MAX_EXACT) / math.log(REL_MAX_DIST / MAX_EXACT)
                         * (N_BUCKETS - MAX_EXACT)).astype(jnp.int32)
    large = jnp.minimum(large, N_BUCKETS - 1)
    return jnp.where(n < MAX_EXACT, n, large)


def _rel_bias(dist, table):
    b = table[_t5_bucket(dist)].astype(jnp.float32)
    b = jnp.moveaxis(b, -1, 0)
    return b.reshape(KV_HEADS, GQA_GROUP, dist.shape[0], dist.shape[1])


def _attend(q, k, v, bias, valid, sinks):
    lead = q.shape[:-3]
    Tq = q.shape[-3]
    qg = q.reshape(*lead, Tq, KV_HEADS, GQA_GROUP, HEAD_DIM)
    s = jnp.einsum("...qkgd,...skd->...kgqs", qg, k).astype(jnp.float32) * ATTN_SCALE + bias
    s = jnp.where(valid, s, NEG_INF)
    sink = sinks.astype(jnp.float32).reshape(KV_HEADS, GQA_GROUP, 1, 1)
    m = jnp.maximum(jnp.max(s, axis=-1, keepdims=True), sink)
    p = jnp.exp(s - m)
    w = p / (jnp.sum(p, axis=-1, keepdims=True) + jnp.exp(sink - m))
    o = jnp.einsum("...kgqs,...skd->...qkgd", w.astype(v.dtype), v)
    return o.reshape(*lead, Tq, N_HEADS, HEAD_DIM)


def _swa_prompt(q, k, v, table, sinks):
    B, S = q.shape[0], q.shape[1]
    nb = S // BLOCK
    qb = q.reshape(B, nb, BLOCK, N_HEADS, HEAD_DIM)
    pad = jnp.zeros((B, BLOCK, KV_HEADS, HEAD_DIM), k.dtype)
    kprev = jnp.concatenate([pad, k[:, :S - BLOCK]], axis=1).reshape(B, nb, BLOCK, KV_HEADS, HEAD_DIM)
    vprev = jnp.concatenate([pad, v[:, :S - BLOCK]], axis=1).reshape(B, nb, BLOCK, KV_HEADS, HEAD_DIM)
    kband = jnp.concatenate([kprev, k.reshape(B, nb, BLOCK, KV_HEADS, HEAD_DIM)], axis=2)
    vband = jnp.concatenate([vprev, v.reshape(B, nb, BLOCK, KV_HEADS, HEAD_DIM)], axis=2)
    qi = jnp.arange(BLOCK)[:, None]
    kj = jnp.arange(2 * BLOCK)[None, :]
    dist = BLOCK + qi - kj
    k_abs = jnp.arange(nb)[:, None, None] * BLOCK - BLOCK + kj
    valid = (dist >= 0) & (dist < WINDOW) & (k_abs >= 0)
    valid = valid[:, None, None]
    out = _attend(qb, kband, vband, _rel_bias(dist, table), valid, sinks)
    return out.reshape(B, S, N_HEADS, HEAD_DIM), k[:, S - CACHE_WIN:], v[:, S - CACHE_WIN:]


def _swa_sample(q, k, v, k_buf, v_buf, table, sinks):
    T = q.shape[1]
    W = k_buf.shape[1]
    kk = jnp.concatenate([k_buf.astype(k.dtype), k], axis=1)
    vv = jnp.concatenate([v_buf.astype(v.dtype), v], axis=1)
    qi = jnp.arange(T)[:, None]
    kj = jnp.arange(W + T)[None, :]
    dist = W + qi - kj
    valid = (dist >= 0) & (dist < WINDOW)
    out = _attend(q, kk, vv, _rel_bias(dist, table), valid, sinks)
    return out, kk[:, kk.shape[1] - W:], vv[:, vv.shape[1] - W:]


def _layer(x, rnn_conv_buf, rnn_h0, k_buf, v_buf, ffn_buf, lw, rel_bias_table, prompt):
    B, T, _ = x.shape
    xn = _rmsnorm(x, lw["norm_mix_g"])
    proj = jnp.einsum("btd,de->bte", xn, lw["w_in"])
    x_rnn, g_rnn, q, k, v = jnp.split(proj, SPLIT_IDX, axis=-1)
    xc, new_rnn_conv = _causal_dwconv(x_rnn, rnn_conv_buf, lw["rnn_conv_w"], lw["rnn_conv_b"])
    h_rnn, new_h = _rglru(xc, rnn_h0, lw["w_gate_a"], lw["b_gate_a"], lw["w_gate_x"], lw["b_gate_x"], lw["rnn_lambda"])
    y_rnn = jax.nn.gelu(g_rnn) * h_rnn
    q = q.reshape(B, T, N_HEADS, HEAD_DIM)
    k = k.reshape(B, T, KV_HEADS, HEAD_DIM)
    v = v.reshape(B, T, KV_HEADS, HEAD_DIM)
    if prompt:
        y_attn, new_k, new_v = _swa_prompt(q, k, v, rel_bias_table, lw["attn_sinks"])
    else:
        y_attn, new_k, new_v = _swa_sample(q, k, v, k_buf, v_buf, rel_bias_table, lw["attn_sinks"])
    merged = jnp.concatenate([_rmsnorm(y_rnn, lw["gn_rnn_g"]),
                              _rmsnorm(y_attn.reshape(B, T, D_ATTN), lw["gn_attn_g"])], axis=-1)
    h = x + jnp.einsum("bte,ed->btd", merged, lw["w_out"])
    hn = _rmsnorm(h, lw["norm_ffn_g"])
    up = jnp.einsum("btd,df->btf", hn, lw["w_up"])
    upc, new_ffn = _causal_dwconv(up, ffn_buf, lw["ffn_conv_w"], lw["ffn_conv_b"])
    gate, val = jnp.split(upc, 2, axis=-1)
    out = h + jnp.einsum("btf,fd->btd", jax.nn.gelu(gate) * val, lw["w_down"])
    return out, (new_rnn_conv, new_h, new_k, new_v, new_ffn)


def setup_inputs(seed: int = 0) -> dict:
    key = jax.random.key(seed)
    ks = jax.random.split(key, 32)
    f = jnp.float32
    nrm = lambda k, shape, s: jax.random.normal(k, shape, f) * s
    a0 = jax.random.uniform(ks[14], (DEPTH, D_RNN), f, minval=0.9, maxval=0.999)
    return {
        "x_prompt": nrm(ks[0], (BATCH, SEQ, D_MODEL), 1.0),
        "x_sample": nrm(ks[1], (DEC_BATCH, DEC_SEQ, D_MODEL), 1.0),
        "state_rnn_conv": nrm(ks[2], (DEPTH, DEC_BATCH, CONV_W - 1, D_RNN), 1.0),
        "state_rnn_h": nrm(ks[3], (DEPTH, DEC_BATCH, D_RNN), 0.5),
        "cache_win_k": nrm(ks[4], (DEPTH, DEC_BATCH, CACHE_WIN, KV_HEADS, HEAD_DIM), 1.0),
        "cache_win_v": nrm(ks[5], (DEPTH, DEC_BATCH, CACHE_WIN, KV_HEADS, HEAD_DIM), 1.0),
        "state_ffn_conv": nrm(ks[6], (DEPTH, DEC_BATCH, FFN_CONV_W - 1, 2 * D_FF), 1.0),
        "norm_mix_g": 1.0 + nrm(ks[7], (DEPTH, D_MODEL), 0.02),
        "w_in": nrm(ks[8], (DEPTH, D_MODEL, D_IN_PROJ), D_MODEL ** -0.5),
        "rnn_conv_w": nrm(ks[9], (DEPTH, CONV_W, D_RNN), CONV_W ** -0.5),
        "rnn_conv_b": nrm(ks[10], (DEPTH, D_RNN), 0.02),
        "w_gate_a": nrm(ks[11], (DEPTH, RNN_BLOCKS, RNN_BLOCK, RNN_BLOCK), RNN_BLOCK ** -0.5),
        "b_gate_a": nrm(ks[12], (DEPTH, D_RNN), 0.02),
        "w_gate_x": nrm(ks[13], (DEPTH, RNN_BLOCKS, RNN_BLOCK, RNN_BLOCK), RNN_BLOCK ** -0.5),
        "b_gate_x": nrm(ks[15], (DEPTH, D_RNN), 0.02),
        "rnn_lambda": jnp.log(a0) - jnp.log1p(-a0),
        "attn_sinks": nrm(ks[16], (DEPTH, N_HEADS), 0.5),
        "rel_bias_table": nrm(ks[17], (N_BUCKETS, N_HEADS), 0.5),
        "gn_rnn_g": 1.0 + nrm(ks[18], (DEPTH, D_RNN), 0.02),
        "gn_attn_g": 1.0 + nrm(ks[19], (DEPTH, D_ATTN), 0.02),
        "w_out": nrm(ks[20], (DEPTH, D_MIX, D_MODEL), D_MIX ** -0.5),
        "norm_ffn_g": 1.0 + nrm(ks[21], (DEPTH, D_MODEL), 0.02),
        "w_up": nrm(ks[22], (DEPTH, D_MODEL, 2 * D_FF), D_MODEL ** -0.5),
        "ffn_conv_w": nrm(ks[23], (DEPTH, FFN_CONV_W, 2 * D_FF), FFN_CONV_W ** -0.5),
        "ffn_conv_b": nrm(ks[24], (DEPTH, 2 * D_FF), 0.02),
        "w_down": nrm(ks[25], (DEPTH, D_FF, D_MODEL), D_FF ** -0.5),
        "norm_final_g": 1.0 + nrm(ks[26], (D_MODEL,), 0.02),
    }


def reference(x_prompt, x_sample, state_rnn_conv, state_rnn_h, cache_win_k, cache_win_v, state_ffn_conv,
              norm_mix_g, w_in, rnn_conv_w, rnn_conv_b, w_gate_a, b_gate_a, w_gate_x, b_gate_x, rnn_lambda,
              attn_sinks, rel_bias_table, gn_rnn_g, gn_attn_g, w_out, norm_ffn_g, w_up, ffn_conv_w, ffn_conv_b,
              w_down, norm_final_g):
    xp, xs = x_prompt, x_sample
    Bp = x_prompt.shape[0]
    dt = x_prompt.dtype
    p_states, s_states = [], []
    for l in range(DEPTH):
        lw = {
            "norm_mix_g": norm_mix_g[l], "w_in": w_in[l], "rnn_conv_w": rnn_conv_w[l], "rnn_conv_b": rnn_conv_b[l],
            "w_gate_a": w_gate_a[l], "b_gate_a": b_gate_a[l], "w_gate_x": w_gate_x[l], "b_gate_x": b_gate_x[l],
            "rnn_lambda": rnn_lambda[l], "attn_sinks": attn_sinks[l], "gn_rnn_g": gn_rnn_g[l],
            "gn_attn_g": gn_attn_g[l], "w_out": w_out[l], "norm_ffn_g": norm_ffn_g[l], "w_up": w_up[l],
            "ffn_conv_w": ffn_conv_w[l], "ffn_conv_b": ffn_conv_b[l], "w_down": w_down[l],
        }
        xp, st_p = _layer(xp,
                          jnp.zeros((Bp, CONV_W - 1, D_RNN), dt),
                          jnp.zeros((Bp, D_RNN), dt),
                          None, None,
                          jnp.zeros((Bp, FFN_CONV_W - 1, 2 * D_FF), dt),
                          lw, rel_bias_table, True)
        xs, st_s = _layer(xs, state_rnn_conv[l], state_rnn_h[l], cache_win_k[l], cache_win_v[l],
                          state_ffn_conv[l], lw, rel_bias_table, False)
        p_states.append(st_p)
        s_states.append(st_s)
    y_prompt = _rmsnorm(xp, norm_final_g)
    y_sample = _rmsnorm(xs, norm_final_g)
    ps = [jnp.stack(z, axis=0) for z in zip(*p_states)]
    ss = [jnp.stack(z, axis=0) for z in zip(*s_states)]
    return (y_prompt, y_sample, ps[0], ps[1], ps[2], ps[3], ps[4], ss[0], ss[1], ss[2], ss[3], ss[4])
```

```python
import contextlib
import math
import numpy as np
import jax
import concourse.bass as bass
import concourse.mybir as mybir
from concourse.bass_utils import run_bass_kernel_spmd

F32 = mybir.dt.float32
BF16 = mybir.dt.bfloat16
AF = mybir.ActivationFunctionType
ALU = mybir.AluOpType
AX = mybir.AxisListType

NCORES = 8
D = 2048
DR = 1024
NT = 2048
NBLK = 4
BT = 512
EPS = 1e-6
NEG = -1e30
EPOCH = 12000

C_GMIX, C_GFFN, C_GRNN, C_GATT = 0, 16, 32, 40
C_CW, C_CB, C_BA, C_BX, C_LAM = 48, 80, 88, 96, 104
C_FW, C_FB = 112, 400
NCV = 496


class Tok:
    __slots__ = ("sem", "val", "eng")

    def __init__(self, sem, val, eng):
        self.sem, self.val, self.eng = sem, val, eng


class Sched:
    def __init__(self, nc, stack):
        self.nc = nc
        self.stack = stack
        self.engs = ("pe", "act", "dve", "pool", "sp")
        self.q = {e: [] for e in self.engs}
        self.cnt = {e: 0 for e in self.engs}
        self.esem = {e: None for e in self.engs}
        self.nsem = 0
        self.waited = {e: {} for e in self.engs}
        self.last_w = {}
        self.readers = {}
        self.dsems = {}
        self.pending = {e: [] for e in self.engs}
        self.dtoks = {}

    def new_sem(self, name):
        self.nsem += 1
        return self.stack.enter_context(self.nc.semaphore(f"{name}_{self.nsem}"))

    def _eng_sem(self, e):
        if self.esem[e] is None or self.cnt[e] >= EPOCH:
            self.esem[e] = self.new_sem("e" + e)
            self.cnt[e] = 0
        return self.esem[e]

    def _deps(self, eng, reads, writes):
        toks = []
        for k in reads:
            t = self.last_w.get(k)
            if t is not None:
                toks.append(t)
        for k in writes:
            t = self.last_w.get(k)
            if t is not None:
                toks.append(t)
            toks.extend(self.readers.get(k, ()))
        waits = {}
        for t in toks:
            if t.eng == eng and eng == "pe":
                continue
            assert t.sem is not None, "dependency on unsignaled op"
            key = id(t.sem)
            if self.waited[eng].get(key, 0) >= t.val:
                continue
            if key not in waits or waits[key][1] < t.val:
                waits[key] = (t.sem, t.val)
        for key, (s, v) in waits.items():
            self.waited[eng][key] = v
        return list(waits.values())

    def _commit(self, tok, reads, writes):
        for k in writes:
            self.last_w[k] = tok
            self.readers[k] = []
        for k in reads:
            self.readers.setdefault(k, []).append(tok)

    @staticmethod
    def _split(reads, writes):
        ps = [k for k in reads if isinstance(k, str) and k[:2] in ("mm", "sc", "tp")]
        if ps:
            reads = [k for k in reads if k not in ps]
            writes = list(writes) + ps
        return reads, writes

    def op(self, eng, fn, reads=(), writes=(), signal=True):
        reads, writes = self._split(reads, writes)
        waits = self._deps(eng, reads, writes)
        if signal:
            sem = self._eng_sem(eng)
            self.cnt[eng] += 1
            tok = Tok(sem, self.cnt[eng], eng)
            for p in self.pending[eng]:
                p.sem, p.val = sem, tok.val
            self.pending[eng] = []
        else:
            tok = Tok(None, None, eng)
            self.pending[eng].append(tok)
        self._commit(tok, reads, writes)
        self.q[eng].append((waits, fn, (tok.sem, 1) if signal else None))
        return tok

    def dma(self, qeng, out, in_, reads=(), writes=(), sem=None, **kw):
        waits = self._deps(qeng, reads, writes)
        if sem not in self.dsems:
            self.dsems[sem] = [self.new_sem("d"), 0]
        ds = self.dsems[sem]
        ds[1] += 16
        tok = Tok(ds[0], ds[1], "dma")
        self.dtoks.setdefault(sem, []).append(tok)
        self._commit(tok, reads, writes)
        self.q[qeng].append((waits, lambda e: e.dma_start(out=out, in_=in_, **kw), (ds[0], 16)))
        return tok

    def seal(self, sem):
        ds = self.dsems[sem]
        for t in self.dtoks.get(sem, ()):
            t.val = ds[1]

    def custom(self, qeng, fn, reads=(), writes=(), sem=None, inc=1):
        waits = self._deps(qeng, reads, writes)
        if sem not in self.dsems:
            self.dsems[sem] = [self.new_sem("c"), 0]
        ds = self.dsems[sem]
        ds[1] += inc
        tok = Tok(ds[0], ds[1], "dma")
        self._commit(tok, reads, writes)
        self.q[qeng].append((waits, fn, (ds[0], inc)))
        return tok

    def wait_all(self, eng="sp"):
        waits = []
        for name, (s, c) in self.dsems.items():
            if self.waited[eng].get(id(s), 0) < c:
                waits.append((s, c))
                self.waited[eng][id(s)] = c
        self.q[eng].append((waits, None, None))

    def emit(self):
        assert all(not self.pending[e] for e in self.engs), "unsignaled trailing ops"
        nc = self.nc
        with nc.Block() as block:
            def mk(ename):
                def body(e):
                    for waits, fn, inc in self.q[ename]:
                        for s, v in waits:
                            e.wait_ge(s, v)
                        if fn is not None:
                            ins = fn(e)
                            if inc is not None:
                                ins.then_inc(inc[0], inc[1])
                return body
            block.tensor(mk("pe"))
            block.scalar(mk("act"))
            block.vector(mk("dve"))
            block.gpsimd(mk("pool"))
            block.sync(mk("sp"))


def build(dbg=None, nblk=NBLK, upto='all'):
    nc = bass.Bass("TRN2", target_bir_lowering=False)
    dbg = dbg or {}

    def din(name, shape, dt=F32):
        return nc.dram_tensor(name, list(shape), dt, kind="ExternalInput")

    def dout(name, shape, dt=F32):
        return nc.dram_tensor(name, list(shape), dt, kind="ExternalOutput")

    xp = din("xp", [128 + NT, D])
    w_in = din("w_in", [D, 3584])
    w_out = din("w_out", [D, D])
    w_up = din("w_up", [D, 12288])
    w_down = din("w_down", [6144, D])
    cvec_d = din("cvec", [128, NCV])
    gfin_d = din("gfin", [1, D])
    sinks_d = din("sinks", [1, 16])
    table_d = din("table", [32, 16])
    wga_d = din("wga", [16, 64, 64])
    wgx_d = din("wgx", [16, 64, 64])
    ident_d = din("ident", [128, 128])
    oh2_d = din("oh2", [32, 384])
    mrow_d = din("mrow", [16, 384])
    cflag_d = din("cflag", [128, 32])
    aident_d = din("aident", [128, 128])
    xs_in = din("xs_in", [64, D])
    src_conv_d = din("src_conv", [128, 8, 48])
    srh_d = din("srh", [128, 8, 16])
    kcT_d = din("kcT", [128, 2, 16, 128])
    vc_d = din("vc", [128, 16, 256])
    kc_o = din("kc_o", [16, 128, 256])
    vc_o = din("vc_o", [16, 128, 256])
    sfc_d = din("sfc", [128, 96, 32])
    sinkr_d = din("sinkr", [64, 1])
    yp = dout("yp", [NT, D])
    ys_o = dout("ys_o", [66, D])
    o_src = dout("o_src", [48, DR])
    o_srh = dout("o_srh", [128, 8, 16])
    o_sk = dout("o_sk", [16, 128, 256])
    o_sv = dout("o_sv", [16, 128, 256])
    o_sfc = dout("o_sfc", [32, 12288])
    o_prc = dout("o_prc", [3, DR])
    o_prh = dout("o_prh", [128, 8])
    o_pk = dout("o_pk", [128, 256])
    o_pv = dout("o_pv", [128, 256])
    o_pfc = dout("o_pfc", [2, 12288])
    tr_d = nc.dram_tensor("tr_scratch", [16, 384], F32)
    ex_in = nc.dram_tensor("ex_in", [128, 16], F32)
    ex2_in = nc.dram_tensor("ex2_in", [128, 192], F32)
    ex2_out = nc.dram_tensor("ex2_out", [NCORES * 128, 192], F32)
    hfix_d = nc.dram_tensor("hfix_d", [2, D], F32)
    oscr = nc.dram_tensor("oscr", [64, 16, 256], F32)
    vscr = nc.dram_tensor("vscr", [64, 256], BF16)
    ex_out = nc.dram_tensor("ex_out", [NCORES * 128, 16], F32)
    dbg_out = {}

    with contextlib.ExitStack() as st:
        S = Sched(nc, st)

        def sb(name, shape, dt=F32):
            return st.enter_context(nc.sbuf_tensor("s_" + name, list(shape), dt))

        def A(fn, r=(), w=()):
            return S.op("act", fn, r, w)

        def V(fn, r=(), w=()):
            return S.op("dve", fn, r, w)

        def G(fn, r=(), w=()):
            return S.op("pool", fn, r, w)

        def T(fn, r=(), w=(), signal=True):
            return S.op("pe", fn, r, w, signal=signal)

        resid = sb("resid", [128, 4, D])
        fm = sb("fm", [128, 16, BT], BF16)
        big = sb("big", [128, 48 * BT], BF16)
        act = big[:].rearrange("p (f t) -> p f t", t=BT)
        xr = big[:, 0:8240].bitcast(F32).rearrange("p (c t) -> p c t", t=515)
        gg = big[:, 8448:8448 + 8192].bitcast(F32).rearrange("p (c t) -> p c t", t=BT)
        qT = big[:, 16640:16640 + 4096].rearrange("p (c t) -> p c t", t=BT)
        kT = sb("kT", [128, 2, 128 + BT], BF16)
        vtok = sb("vtok", [128, 5, 256], BF16)
        bias = sb("bias", [128, 16, 256])
        xs = sb("xs", [128, D])
        s_half = xs[:].rearrange("p (h t) -> p h t", t=256)
        pbuf = sb("pbuf", [128, 2, 256], BF16)
        ptb = sb("ptb", [128, 2, 2, 128], BF16)
        rt = {n: sb("rt_" + n, [128, BT]) for n in ("xc", "ra", "i", "t", "hb", "ysq")}
        xcb = sb("xcb", [128, BT], BF16)
        rr = sb("rr", [128, BT])
        gacc = sb("gacc", [128, 4, BT])
        vacc = sb("vacc", [128, 2, BT])
        hal = sb("hal", [128, 96, 2])
        slabs = [sb(f"slab{i}", [128, 16, 512], BF16) for i in range(2)]
        wbd = [sb(f"wbd{i}", [128, 8, 128], BF16) for i in range(2)]
        ident = sb("ident", [128, 128])
        ones_f = sb("ones_f", [128, 128])
        cvec = sb("cvec", [128, NCV])
        sinks_b = sb("sinks_b", [128, 16])
        cflag = sb("cflag", [128, 32])
        cneg = sb("cneg", [128, 8])
        hstate = sb("hstate", [128, 8])
        xrt = sb("xrt", [128, 8, 3])
        first2 = sb("first2", [128, 96, 2])
        xpfix = sb("xpfix", [128, 96, 4])
        exh = [sb(f"exh{i}", [128, 192]) for i in range(2)]
        bias_s = sb("bias_s", [64, 132])
        sinkr = sb("sinkr", [64, 1])
        sst = sb("sst", [64, 16])
        xpf = [sb(f"xpf{i}", [128, 96]) for i in range(2)]
        accs = [sb(f"accs{i}", [128, 66]) for i in range(3)]
        rsum = sb("rsum", [128, 8])
        rtot = sb("rtot", [128, 8])
        exs = sb("exs", [128, 16])
        exg = sb("exg", [128, 8, 16])
        exa = sb("exa", [128, 8])
        kvf = sb("kvf", [128, 512])
        stat = sb("stat", [128, 64])
        m8 = sb("m8", [128, 16])
        negm = sb("negm", [128, 16])
        rs = sb("rs", [128, 16])
        rden = sb("rden", [128, 16])
        es = sb("es", [128, 16])
        tab_sb = sb("tab_sb", [32, 16])
        oh2_sb = gacc[0:32, 0, 0:384]
        tr_sb = gacc[0:16, 1, 0:384]
        mrow_sb = gacc[0:16, 2, 0:384]
        small_o = xs
        aident = sb("aident", [128, 128])

        mm = st.enter_context(nc.psum_tensor("mm", [128, 4, 512], F32))
        sc = st.enter_context(nc.psum_tensor("sc", [128, 2, 512], F32))
        tp = st.enter_context(nc.psum_tensor("tp", [128, 2, 512], F32))
        tpb = tp[:, 1, :].bitcast(BF16)

        cq = "sp"
        S.dma(cq, ident[:], ident_d[:, :], writes=["ident"], sem="const")
        S.dma(cq, cvec[:], cvec_d[:, :], writes=["cvec"], sem="const")
        S.dma(cq, sinks_b[:], sinks_d[0:1, :].partition_broadcast(128), writes=["sinks_b"], sem="const")
        S.dma(cq, cflag[:], cflag_d[:, :], writes=["cflag"], sem="const")
        S.dma(cq, tab_sb[:], table_d[:, :], writes=["tab"], sem="const")
        S.dma(cq, oh2_sb, oh2_d[:, :], writes=["oh2", ("gacc", 0)], sem="const")
        S.dma(cq, mrow_sb, mrow_d[:, :], writes=["mrow", ("gacc", 2)], sem="const")
        S.dma(cq, aident[:], aident_d[:, :], writes=["aident"], sem="const")
        S.seal("const")
        V(lambda e: e.memset(ones_f[:], 1.0), w=["ones"])
        V(lambda e: e.memset(hstate[:], 0.0), w=["hstate"])
        V(lambda e: e.memset(hal[:], 0.0), w=["hal"])
        for gi, (wd, key) in enumerate(((wga_d, "wbd0"), (wgx_d, "wbd1"))):
            G(lambda e, gi=gi: e.memset(wbd[gi][:], 0.0), w=[key])
            wv = wd.ap().rearrange("(c two) i j -> two i c j", two=2)
            for par in range(2):
                S.dma("pool", wbd[gi][par * 64:(par + 1) * 64, :, par * 64:(par + 1) * 64], wv[par],
                      writes=[key], sem="constp")
        S.seal("constp")
        A(lambda e: e.activation(cneg[:], cvec[:, C_LAM:C_LAM + 8], AF.Exp, scale=-1.0), r=["cvec"], w=["cneg"])
        A(lambda e: e.activation(cneg[:], cneg[:], AF.Ln, bias=1.0), r=["cneg"], w=["cneg"])
        V(lambda e: e.tensor_scalar(cneg[:], cneg[:], -8.0, None, ALU.mult), r=["cneg"], w=["cneg"])

        T(lambda e: e.matmul(mm[0:16, 0, 0:384], lhsT=tab_sb[:], rhs=oh2_sb, start=True, stop=True),
          r=["tab", "oh2", ("gacc", 0)], w=["mm0"])
        V(lambda e: e.tensor_tensor(tr_sb, mm[0:16, 0, 0:384], mrow_sb, ALU.add), r=["mm0", "mrow", ("gacc", 2)], w=["tr_sb", ("gacc", 1)])
        S.dma("sp", tr_d[:, :], tr_sb, reads=["tr_sb", ("gacc", 1)], writes=["tr_d"], sem="const")
        wwin = big[:, 8448:8448 + 8192].bitcast(F32).rearrange("p (h s) -> p h s", s=256)
        ggk = [("gg", c) for c in range(8)]
        S.dma("sp", wwin, bass.AP(tr_d, 0, [[1, 128], [384, 16], [1, 256]]), reads=["tr_d"], writes=ggk, sem="const")
        for h in range(16):
            b = h % 2
            T(lambda e, h=h, b=b: e.matmul(sc[:, b, 0:256], lhsT=aident[:], rhs=wwin[:, h, :], start=True, stop=True),
              r=ggk + ["aident"], w=[f"sc{b}"])
            V(lambda e, h=h, b=b: e.tensor_copy(bias[:, h, :], sc[:, b, 0:256]), r=[f"sc{b}"], w=["bias"])

        def wslab(w, r0, c0):
            return (w, r0, c0)

        def slab_ap(src):
            w, r0, c0 = src
            return w.ap()[r0:r0 + 2048, c0:c0 + 512].rearrange("(k p) c -> p k c", p=128)

        plan = []
        if dbg.get("prepass", True):
            for blk_ in range(NBLK):
                if blk_ == 0:
                    plan += [(w_in, 0, s * 512) for s in (0, 1)]
                plan += [(w_in, 0, s * 512) for s in (0, 1)]
        for blk_ in range(NBLK):
            if blk_ == 0:
                plan += [(w_in, 0, s * 512) for s in (0, 1, 6)]
            plan += [(w_in, 0, s * 512) for s in range(7)]
            plan += [(w_out, 0, dg * 512) for dg in range(4)]
            plan += [(w_up, 0, isval * 6144 + sp_ * 512) for sp_ in range(12) for isval in range(2)]
            plan += [(w_down, rg * 2048, dg * 512) for dg in range(4) for rg in range(3)]
        if dbg.get("sample", True):
            plan += [(w_in, 0, s_ * 512) for s_ in range(7)]
            plan += [(w_out, 0, dg * 512) for dg in range(4)]
            plan += [(w_up, 0, isval * 6144 + sp_ * 512) for sp_ in range(12) for isval in range(2)]
            plan += [(w_down, rg * 2048, dg * 512) for dg in range(4) for rg in range(3)]
        slab_state = {"next_use": 0, "next_load": 0}

        def _issue_load():
            k = slab_state["next_load"]
            if k >= len(plan):
                return
            i = k % 2
            w_, r0_, c0_ = plan[k]
            if False:
                for a_ in range(2):
                    for g_ in range(4):
                        cs = c0_ + a_ * 256 + g_ * 64
                        src = w_.ap()[r0_:r0_ + 2048, cs:cs + 64].rearrange("(k p) d -> p k d", p=128)
                        dst = slabs[i][:, :, g_ * 128 + a_ * 64:g_ * 128 + a_ * 64 + 64]
                        S.dma("pool", dst, src, writes=[f"slab{i}"], sem=f"slab{i}")
            else:
                S.dma("pool", slabs[i][:], slab_ap(plan[k]), writes=[f"slab{i}"], sem=f"slab{i}")
            slab_state["next_load"] = k + 1

        def load_slab(src):
            k = slab_state["next_use"]
            assert plan[k][1:] == src[1:] and plan[k][0] is src[0], (k, plan[k][1:], src[1:])
            while slab_state["next_load"] <= k:
                _issue_load()
            slab_state["next_use"] = k + 1
            return k % 2

        def prefetch_next():
            if slab_state["next_load"] <= slab_state["next_use"]:
                _issue_load()

        mm_rot = [0]

        def next_mm(n=3):
            i = mm_rot[0] % n
            mm_rot[0] += 1
            return i

        tp_rot = [0]

        def norm_to_fm(src_tile_ap, src_keys, gcol, tile, dst=fm, dst_keyf=None, ntok=128):
            n = ntok
            A(lambda e: e.activation(xs[0:n, :], src_tile_ap, AF.Square, accum_out=stat[0:n, 0:1]),
              r=src_keys, w=["xs", "stat0"])
            A(lambda e: e.activation(stat[0:n, 1:2], stat[0:n, 0:1], AF.Sqrt, scale=1.0 / D, bias=EPS),
              r=["stat0"], w=["stat1"])
            V(lambda e: e.reciprocal(stat[0:n, 2:3], stat[0:n, 1:2]), r=["stat1"], w=["stat2"])
            V(lambda e: e.tensor_scalar(xs[0:n, :], src_tile_ap, stat[0:n, 2:3], None, ALU.mult),
              r=list(src_keys) + ["stat2"], w=["xs"])
            for grp in range(4):
                b = tp_rot[0] % 2
                tp_rot[0] += 1
                for c4 in range(4):
                    c = grp * 4 + c4
                    T(lambda e, b=b, c4=c4, c=c: e.transpose(tp[:, b, c4 * 128:c4 * 128 + n], xs[0:n, c * 128:(c + 1) * 128], ident[0:n, 0:n]),
                      r=["xs", "ident"], w=[f"tp{b}"], signal=(c4 == 3))
                dk = [dst_keyf(c, tile) for c in range(grp * 4, grp * 4 + 4)]
                V(lambda e, b=b, grp=grp: e.tensor_tensor(
                    dst[:, grp * 4:(grp + 1) * 4, tile * 128:tile * 128 + n],
                    tp[:, b, :].rearrange("p (c t) -> p c t", t=128)[:, :, 0:n],
                    cvec[:, gcol + grp * 4:gcol + grp * 4 + 4].unsqueeze(2).to_broadcast([128, 4, n]), ALU.mult),
                  r=[f"tp{b}", "cvec"], w=dk)

        def dump(tag, ap, keys):
            if not dbg.get("dump"):
                return
            shp = list(ap.shape)
            d_ = nc.dram_tensor("dbg_" + tag, shp, ap.dtype, kind="ExternalOutput")
            S.dma("sp", d_.ap(), ap, reads=keys, sem="dbg_" + tag)

        def fmk(c, t):
            return ("fm", c, t)

        def fm_all(c):
            return [("fm", c, t) for t in range(4)]

        def prompt_block(blk, pre=False):
            last = blk == NBLK - 1
            if blk > 0:
                if not pre:
                    V(lambda e: e.tensor_copy(kT[:, :, 0:128], kT[:, :, BT:BT + 128]), r=["kT"], w=["kTh"])
                    V(lambda e: e.tensor_copy(vtok[:, 0, :], vtok[:, 4, :]), r=[("vtok", 4)], w=[("vtok", 0)])
                V(lambda e: e.tensor_copy(xr[:, :, 0:3], xrt[:]), r=["xrt"], w=["xrh"] + [("act", f) for f in range(17)])
            if blk == 0:
                S.dma("sp", resid[:, 3, :], xp[0:128, :], writes=[("resid", 3)], sem="x3")
                norm_to_fm(resid[:, 3, :], [("resid", 3)], C_GMIX, 3, dst_keyf=fmk)
            for t in range(4):
                if blk == 0 and t == 3:
                    continue
                r0 = 128 + blk * BT + t * 128
                S.dma("sp", resid[:, t, :], xp[r0:r0 + 128, :], writes=[("resid", t)], sem=f"x{t}")
            if blk == 0:
                halo_inproj((0, 1) if pre else (0, 1, 6))
                r0 = 128 + 3 * 128
                S.dma("sp", resid[:, 3, :], xp[r0:r0 + 128, :], writes=[("resid", 3)], sem="x3")
            for t in range(4):
                norm_to_fm(resid[:, t, :], [("resid", t)], C_GMIX, t, dst_keyf=fmk)
            if pre:
                inproj(blk, pre=True)
                return
            if blk == 0:
                dump("xnT", fm[:], [fmk(c, t) for c in range(16) for t in range(4)])
                dump("stat", stat[:], ["stat0", "stat1", "stat2"])
                dump("xs", xs[:], ["xs"])
                dump("x3", resid[:, 3, :], [("resid", 3)])
            if upto == 'norm':
                return
            inproj(blk)
            if blk == 0:
                dump("qT", qT, [("qT", c) for c in range(8)])
                dump("kT", kT[:], ["kT", "kTh"])
                dump("vtok", vtok[:], [("vtok", i) for i in range(5)])
                dump("yrnn", gg, [("gg", c) for c in range(8)])
                dump("mrnn", fm[:, 0:8, :], [fmk(c, t) for c in range(8) for t in range(4)])
                dump("bias", bias[:], ["bias"])
            if upto == 'inproj':
                return
            for t in range(4):
                attention(blk, t)
            if blk == 0:
                dump("merged", fm[:], [fmk(c, t) for c in range(16) for t in range(4)])
            if upto == 'attn':
                return
            outproj()
            if blk == 0:
                S.dma("sp", hfix_d[:, :], resid[0:2, 0, :], reads=[("resid", 0)], writes=["hfix_d"], sem="hfix")
                dump("h", resid[:], [("resid", t) for t in range(4)])
            if upto == 'outproj':
                return
            for t in range(4):
                norm_to_fm(resid[:, t, :], [("resid", t)], C_GFFN, t, dst_keyf=fmk)
            upproj(blk)
            if blk == 0:
                dump("act", act, [("act", f) for f in range(48)])
            if upto == 'up':
                return
            if blk == NBLK - 1 and dbg.get("sample", True):
                S.dma("sp", ex2_in[:, :], hal[:].rearrange("p f t -> p (f t)"), reads=[("hal", f) for f in range(96)], writes=["ex2_in"], sem="ex")
                S.custom("pool", lambda e: e.collective_compute("AllGather", ALU.bypass, replica_groups=[list(range(NCORES))],
                                                                ins=[ex2_in.ap().opt()], outs=[ex2_out.ap().opt()]),
                         reads=["ex2_in"], writes=["ex2_out"], sem="cc2", inc=1)
            S.dma("sp", xs[:], gfin_d[0:1, :].partition_broadcast(128), writes=["xs"], sem="gfin")
            downproj(blk)

        def halo_inproj(slist):
            hc = slice(3 * 128, 4 * 128)
            for s in slist:
                i = load_slab(wslab(w_in, 0, s * 512))
                sk = f"slab{i}"
                if s < 2:
                    b = next_mm()
                    for j in range(4):
                        for k in range(16):
                            T(lambda e, b=b, j=j, k=k, i=i: e.matmul(mm[:, b, j * 4:j * 4 + 3], lhsT=slabs[i][:, k, j * 128:(j + 1) * 128],
                                                                      rhs=fm[:, k, 3 * 128 + 125:3 * 128 + 128], start=(k == 0), stop=(k == 15)),
                              r=[sk, fmk(k, 3)], w=[f"mm{b}"], signal=(k == 15))
                    V(lambda e, b=b, s=s: e.tensor_copy(xr[:, s * 4:(s + 1) * 4, 0:3],
                                                        mm[:, b, 0:16].rearrange("p (j t) -> p j t", t=4)[:, :, 0:3]),
                      r=[f"mm{b}"], w=["xrh"])
                else:
                    for kp in range(2):
                        b = next_mm()
                        for k in range(16):
                            T(lambda e, b=b, kp=kp, k=k, i=i: e.matmul(mm[:, b, 0:128], lhsT=slabs[i][:, k, kp * 128:(kp + 1) * 128],
                                                                        rhs=fm[:, k, hc], start=(k == 0), stop=(k == 15)),
                              r=[sk, fmk(k, 3)], w=[f"mm{b}"], signal=(k == 15))
                        A(lambda e, b=b, kp=kp: e.copy(kT[:, kp, 0:128], mm[:, b, 0:128]), r=[f"mm{b}"], w=["kTh"])
                    b = next_mm()
                    for k in range(16):
                        T(lambda e, b=b, k=k, i=i: e.matmul(mm[:, b, :], lhsT=fm[:, k, hc], rhs=slabs[i][:, k, :],
                                                             start=(k == 0), stop=(k == 15)),
                          r=[sk, fmk(k, 3)], w=[f"mm{b}"], signal=(k == 15))
                    V(lambda e, b=b: e.tensor_copy(vtok[:, 0, :], mm[:, b, 256:512]), r=[f"mm{b}"], w=[("vtok", 0)])

        def inproj(blk, pre=False):
            last = (blk == NBLK - 1) and not dbg.get("nolast")
            for s in (range(2) if pre else range(7)):
                i = load_slab(wslab(w_in, 0, s * 512))
                sk = f"slab{i}"
                if s < 6:
                    for j in range(4):
                        b = next_mm()
                        if s in (4, 5):
                            for k in range(16):
                                for a_ in range(2):
                                    hc = (a_ * 4 + j) * 64
                                    T(lambda e, b=b, k=k, a_=a_, hc=hc, i=i: e.matmul(mm[a_ * 64:(a_ + 1) * 64, b, :], lhsT=slabs[i][:, k, hc:hc + 64],
                                                                                       rhs=fm[:, k, :], start=(k == 0), stop=(k == 15)),
                                      r=[sk] + fm_all(k), w=[f"mm{b}"], signal=(k == 15 and a_ == 1))
                        else:
                            for k in range(16):
                                T(lambda e, b=b, k=k, j=j, i=i: e.matmul(mm[:, b, :], lhsT=slabs[i][:, k, j * 128:(j + 1) * 128], rhs=fm[:, k, :],
                                                                          start=(k == 0), stop=(k == 15)),
                                  r=[sk] + fm_all(k), w=[f"mm{b}"], signal=(k == 15))
                        c = (s % 2) * 4 + j
                        if s < 2:
                            A(lambda e, b=b, c=c: e.copy(xr[:, c, 3:3 + BT], mm[:, b, :]), r=[f"mm{b}"], w=[("xr", c)])
                        elif s < 4:
                            A(lambda e, b=b, c=c: e.activation(gg[:, c, :], mm[:, b, :], AF.Gelu), r=[f"mm{b}"], w=[("gg", c)])
                        else:
                            A(lambda e, b=b, c=c: e.activation(qT[:, c, :], mm[:, b, :], AF.Copy, scale=0.125), r=[f"mm{b}"], w=[("qT", c)])
                    if s == 1:
                        V(lambda e: e.tensor_copy(xrt[:], xr[:, :, BT:BT + 3]), r=[("xr", c) for c in range(8)], w=["xrt"])
                    if pre and s == 1:
                        for c in range(8):
                            rnn_chunk(c, main=False)
                    if s in (2, 3, 4, 5):
                        for c in (2 * (s - 2), 2 * (s - 2) + 1):
                            rnn_chunk(c, main=True)
                else:
                    for kp in range(2):
                        b = next_mm()
                        for k in range(16):
                            T(lambda e, b=b, kp=kp, k=k, i=i: e.matmul(mm[:, b, :], lhsT=slabs[i][:, k, kp * 128:(kp + 1) * 128],
                                                                        rhs=fm[:, k, :], start=(k == 0), stop=(k == 15)),
                              r=[sk] + fm_all(k), w=[f"mm{b}"], signal=(k == 15))
                        A(lambda e, b=b, kp=kp: e.copy(kT[:, kp, 128:128 + BT], mm[:, b, :]), r=[f"mm{b}"], w=["kT"])
                    for t in range(4):
                        b = next_mm()
                        for k in range(16):
                            T(lambda e, b=b, k=k, t=t, i=i: e.matmul(mm[:, b, :], lhsT=fm[:, k, t * 128:(t + 1) * 128], rhs=slabs[i][:, k, :],
                                                                      start=(k == 0), stop=(k == 15)),
                              r=[sk, fmk(k, t)], w=[f"mm{b}"], signal=(k == 15))
                        V(lambda e, b=b, t=t: e.tensor_copy(vtok[:, 1 + t, :], mm[:, b, 256:512]), r=[f"mm{b}"], w=[("vtok", 1 + t)])
                        if last and t == 3:
                            A(lambda e, b=b: e.copy(kvf[:], mm[:, b, :]), r=[f"mm{b}"], w=["kvf"])
                            S.dma("sp", o_pk[:, :], kvf[:, 0:256], reads=["kvf"], sem="o_pk")
                            S.dma("sp", o_pv[:, :], kvf[:, 256:512], reads=["kvf"], sem="o_pv")
            if not pre:
                rnn_finish()

        def rnn_chunk(c, main):
            xc, ra, ib, tb, hb, ysq = (rt[n] for n in ("xc", "ra", "i", "t", "hb", "ysq"))
            cw = lambda k: cvec[:, C_CW + k * 8 + c:C_CW + k * 8 + c + 1]
            xk = ("xr", c)
            A(lambda e: e.activation(xc[:], xr[:, c, 3:3 + BT], AF.Identity, scale=cw(3), bias=cvec[:, C_CB + c:C_CB + c + 1]),
              r=[xk, "cvec"], w=["xc"])
            V(lambda e: e.scalar_tensor_tensor(xc[:], xr[:, c, 2:2 + BT], cw(2), xc[:], ALU.mult, ALU.add), r=[xk, "xrh", "xc"], w=["xc"])
            V(lambda e: e.scalar_tensor_tensor(xc[:], xr[:, c, 1:1 + BT], cw(1), xc[:], ALU.mult, ALU.add), r=[xk, "xrh", "xc"], w=["xc"])
            V(lambda e: e.scalar_tensor_tensor(xc[:], xr[:, c, 0:BT], cw(0), xc[:], ALU.mult, ALU.add), r=[xk, "xrh", "xc"], w=["xc"])
            G(lambda e: e.tensor_copy(xcb[:], xc[:]), r=["xc"], w=["xcb"])
            T(lambda e: e.matmul(sc[:, 0, :], lhsT=wbd[0][:, c, :], rhs=xcb[:], start=True, stop=True), r=["xcb", "wbd0"], w=["sc0"])
            T(lambda e: e.matmul(sc[:, 1, :], lhsT=wbd[1][:, c, :], rhs=xcb[:], start=True, stop=True), r=["xcb", "wbd1"], w=["sc1"])
            if main:
                A(lambda e: e.activation(ra[:], sc[:, 0, :], AF.Sigmoid, bias=cvec[:, C_BA + c:C_BA + c + 1]), r=["sc0", "cvec"], w=["ra"])
            else:
                A(lambda e: e.activation(ra[:], sc[:, 0, :], AF.Sigmoid, bias=cvec[:, C_BA + c:C_BA + c + 1], accum_out=rsum[:, c:c + 1]),
                  r=["sc0", "cvec"], w=["ra", "rsum"])
                V(lambda e: e.tensor_tensor(rtot[:, c:c + 1], rtot[:, c:c + 1], rsum[:, c:c + 1], ALU.add), r=["rsum", "rtot"], w=["rtot"])
            A(lambda e: e.activation(ib[:], sc[:, 1, :], AF.Sigmoid, bias=cvec[:, C_BX + c:C_BX + c + 1]), r=["sc1", "cvec"], w=["ib"])
            A(lambda e: e.activation(ra[:], ra[:], AF.Exp, scale=cneg[:, c:c + 1]), r=["ra", "cneg"], w=["ra"])
            V(lambda e: e.tensor_tensor(tb[:], ra[:], ra[:], ALU.mult), r=["ra"], w=["tb"])
            A(lambda e: e.activation(tb[:], tb[:], AF.Sqrt, scale=-1.0, bias=1.0), r=["tb"], w=["tb"])
            G(lambda e: e.tensor_tensor(ib[:], ib[:], xc[:], ALU.mult), r=["ib", "xc"], w=["ib"])
            V(lambda e: e.tensor_tensor(ib[:], ib[:], tb[:], ALU.mult), r=["ib", "tb"], w=["ib"])
            V(lambda e: e.tensor_tensor_scan(hb[:], ra[:], ib[:], hstate[:, c:c + 1], ALU.mult, ALU.add), r=["ra", "ib", "hstate"], w=["hb"])
            V(lambda e: e.tensor_copy(hstate[:, c:c + 1], hb[:, BT - 1:BT]), r=["hb"], w=["hstate"])
            if main:
                G(lambda e: e.tensor_tensor(gg[:, c, :], gg[:, c, :], hb[:], ALU.mult), r=[("gg", c), "hb"], w=[("gg", c)])
                G(lambda e: e.tensor_tensor(ysq[:], gg[:, c, :], gg[:, c, :], ALU.mult), r=[("gg", c)], w=["ysq"])
                T(lambda e: e.matmul(mm[:, 3, :], lhsT=ones_f[:], rhs=ysq[:], start=(c == 0), stop=(c == 7)),
                  r=["ysq", "ones"], w=["mm3"], signal=True)

        def rnn_finish():
            A(lambda e: e.activation(rr[:], mm[:, 3, :], AF.Sqrt, scale=1.0 / DR, bias=EPS), r=["mm3"], w=["rr"])
            V(lambda e: e.reciprocal(rr[:], rr[:]), r=["rr"], w=["rr"])
            for c in range(8):
                V(lambda e, c=c: e.scalar_tensor_tensor(fm[:, c, :], gg[:, c, :], cvec[:, C_GRNN + c:C_GRNN + c + 1], rr[:], ALU.mult, ALU.mult),
                  r=[("gg", c), "rr", "cvec"], w=fm_all(c))

        def attention(blk, t):
            first = (blk == 0 and t == 0)
            o_ps = mm[:, 0:2, :].rearrange("p a b -> p (a b)")
            for kp in range(2):
                for g in range(4):
                    for half in range(2):
                        h = 8 * kp + 4 * half + g
                        idx = 4 * half + g
                        ps = slice(half * 64, (half + 1) * 64)
                        T(lambda e, half=half, g=g, ps=ps, kp=kp: e.matmul(
                            sc[:, half, 0:256], lhsT=qT[ps, kp * 4 + g, t * 128:(t + 1) * 128], rhs=kT[ps, kp, t * 128:t * 128 + 256],
                            start=True, stop=True),
                          r=[("qT", kp * 4 + g), "kT", "kTh"], w=[f"sc{half}"])
                        V(lambda e, half=half, idx=idx, h=h: e.tensor_tensor(s_half[:, idx, :], sc[:, half, 0:256], bias[:, h, :], ALU.add),
                          r=[f"sc{half}", "bias"], w=["xs"])
                if first:
                    V(lambda e: e.tensor_scalar(s_half[:, :, 0:128], s_half[:, :, 0:128], cflag[:, 9:10], None, ALU.add), r=["xs", "cflag"], w=["xs"])
                hs = slice(8 * kp, 8 * kp + 8)
                V(lambda e, hs=hs: e.tensor_reduce(m8[:, hs], s_half[:], AX.X, ALU.max), r=["xs"], w=["m8"])
                V(lambda e, hs=hs: e.tensor_tensor(m8[:, hs], m8[:, hs], sinks_b[:, hs], ALU.max), r=["m8", "sinks_b"], w=["m8"])
                V(lambda e, hs=hs: e.tensor_scalar(negm[:, hs], m8[:, hs], -1.0, None, ALU.mult), r=["m8"], w=["negm"])
                V(lambda e, hs=hs: e.tensor_tensor(es[:, hs], sinks_b[:, hs], m8[:, hs], ALU.subtract), r=["m8", "sinks_b"], w=["es"])
                A(lambda e, hs=hs: e.activation(es[:, hs], es[:, hs], AF.Exp), r=["es"], w=["es"])
                for idx in range(8):
                    h = 8 * kp + idx
                    kvh = h // 4
                    pi = idx % 2
                    A(lambda e, idx=idx, h=h, pi=pi: e.activation(pbuf[:, pi, :], s_half[:, idx, :], AF.Exp, bias=negm[:, h:h + 1],
                                                                   accum_out=rs[:, h:h + 1]),
                      r=["xs", "negm"], w=[("pbuf", pi), ("rs", h)])
                    tpx = mm[:, 2 + pi, :].bitcast(BF16)
                    for kb in range(2):
                        T(lambda e, pi=pi, kb=kb, tpx=tpx: e.transpose(tpx[:, kb * 128:(kb + 1) * 128], pbuf[:, pi, kb * 128:(kb + 1) * 128],
                                                                       ident_b[:]),
                          r=[("pbuf", pi), "ident_b"], w=[f"mm{2 + pi}"], signal=(kb == 1))
                    V(lambda e, pi=pi, tpx=tpx: e.tensor_copy(ptb[:, pi, :, :], tpx[:, 0:256].rearrange("p (a b) -> p a b", b=128)),
                      r=[f"mm{2 + pi}"], w=[("ptb", pi)])
                    for kb in range(2):
                        T(lambda e, pi=pi, kb=kb, h=h, kvh=kvh: e.matmul(o_ps[:, h * 64:(h + 1) * 64], lhsT=ptb[:, pi, kb, :],
                                                                          rhs=vtok[:, t + kb, kvh * 64:(kvh + 1) * 64], start=(kb == 0), stop=(kb == 1)),
                          r=[("ptb", pi), ("vtok", t + kb)], w=["mm0", "mm1"], signal=(kb == 1))
                V(lambda e, hs=hs: e.tensor_tensor(rden[:, hs], rs[:, hs], es[:, hs], ALU.add), r=[("rs", h) for h in range(8 * kp, 8 * kp + 8)] + ["es"], w=["rden"])
                V(lambda e, hs=hs: e.reciprocal(rden[:, hs], rden[:, hs]), r=["rden"], w=["rden"])
            yat = xs[:, 0:1024]
            V(lambda e: e.tensor_tensor(yat.rearrange("p (h d) -> p h d", d=64), o_ps.rearrange("p (h d) -> p h d", d=64),
                                        rden[:].unsqueeze(2).to_broadcast([128, 16, 64]), ALU.mult),
              r=["mm0", "mm1", "rden"], w=["xs"])
            A(lambda e: e.activation(xs[:, 1024:2048], yat, AF.Square, accum_out=stat[:, 4:5]), r=["xs"], w=["xs2", "stat4"])
            A(lambda e: e.activation(stat[:, 5:6], stat[:, 4:5], AF.Sqrt, scale=1.0 / DR, bias=EPS), r=["stat4"], w=["stat5"])
            V(lambda e: e.reciprocal(stat[:, 6:7], stat[:, 5:6]), r=["stat5"], w=["stat6"])
            V(lambda e: e.tensor_scalar(yat, yat, stat[:, 6:7], None, ALU.mult), r=["xs", "stat6"], w=["xs"])
            for grp in range(2):
                for c4 in range(4):
                    c = grp * 4 + c4
                    T(lambda e, c4=c4, c=c: e.transpose(tp[:, 0, c4 * 128:(c4 + 1) * 128], xs[:, c * 128:(c + 1) * 128], ident[:]),
                      r=["xs", "ident"], w=["tp0"], signal=(c4 == 3))
                V(lambda e, grp=grp: e.tensor_tensor(
                    fm[:, 8 + grp * 4:8 + (grp + 1) * 4, t * 128:(t + 1) * 128],
                    tp[:, 0, :].rearrange("p (c t) -> p c t", t=128),
                    cvec[:, C_GATT + grp * 4:C_GATT + grp * 4 + 4].unsqueeze(2).to_broadcast([128, 4, 128]), ALU.mult),
                  r=["tp0", "cvec"], w=[fmk(8 + grp * 4 + c4, t) for c4 in range(4)])

        def outproj():
            for dg in range(4):
                i = load_slab(wslab(w_out, 0, dg * 512))
                sk = f"slab{i}"
                for e_ in range(16):
                    for t in range(4):
                        T(lambda e, e_=e_, t=t, i=i: e.matmul(mm[:, t, :], lhsT=fm[:, e_, t * 128:(t + 1) * 128], rhs=slabs[i][:, e_, :],
                                                               start=(e_ == 0), stop=(e_ == 15)),
                          r=[sk, fmk(e_, t)], w=[f"mm{t}"], signal=(e_ == 15))
                for t in range(4):
                    V(lambda e, t=t, dg=dg: e.tensor_tensor(resid[:, t, dg * 512:(dg + 1) * 512], mm[:, t, :], resid[:, t, dg * 512:(dg + 1) * 512], ALU.add),
                      r=[f"mm{t}", ("resid", t)], w=[("resid", t)])

        def upproj(blk):
            last = blk == NBLK - 1
            for sp_ in range(12):
                for isval in range(2):
                    i = load_slab(wslab(w_up, 0, isval * 6144 + sp_ * 512))
                    sk = f"slab{i}"
                    for j in range(4):
                        f = isval * 48 + sp_ * 4 + j
                        b = next_mm(4)
                        for k in range(16):
                            T(lambda e, b=b, k=k, j=j, i=i: e.matmul(mm[:, b, :], lhsT=slabs[i][:, k, j * 128:(j + 1) * 128], rhs=fm[:, k, :],
                                                                      start=(k == 0), stop=(k == 15)),
                              r=[sk] + fm_all(k), w=[f"mm{b}"], signal=(k == 15))
                        acc = vacc[:, j % 2, :] if isval else gacc[:, j, :]
                        ak = ("vacc", j % 2) if isval else ("gacc", j)
                        fw = lambda k, f=f: cvec[:, C_FW + k * 96 + f:C_FW + k * 96 + f + 1]
                        mk = f"mm{b}"
                        A(lambda e, b=b, acc=acc, fw=fw, f=f: e.activation(acc, mm[:, b, :], AF.Identity, scale=fw(2), bias=cvec[:, C_FB + f:C_FB + f + 1]),
                          r=[mk, "cvec"], w=[ak])
                        V(lambda e, b=b, acc=acc, fw=fw: e.scalar_tensor_tensor(acc[:, 1:BT], mm[:, b, 0:BT - 1], fw(1), acc[:, 1:BT], ALU.mult, ALU.add),
                          r=[mk, ak], w=[ak])
                        V(lambda e, b=b, acc=acc, fw=fw: e.scalar_tensor_tensor(acc[:, 2:BT], mm[:, b, 0:BT - 2], fw(0), acc[:, 2:BT], ALU.mult, ALU.add),
                          r=[mk, ak], w=[ak])
                        V(lambda e, acc=acc, fw=fw, f=f: e.scalar_tensor_tensor(acc[:, 0:2], hal[:, f, :], fw(0), acc[:, 0:2], ALU.mult, ALU.add),
                          r=[("hal", f), ak], w=[ak])
                        V(lambda e, acc=acc, fw=fw, f=f: e.scalar_tensor_tensor(acc[:, 0:1], hal[:, f, 1:2], fw(1), acc[:, 0:1], ALU.mult, ALU.add),
                          r=[("hal", f), ak], w=[ak])
                        V(lambda e, b=b, f=f: e.tensor_copy(hal[:, f, :], mm[:, b, BT - 2:BT]), r=[mk], w=[("hal", f)])
                        if blk == 0:
                            V(lambda e, b=b, f=f: e.tensor_copy(xpfix[:, f, 2:4], mm[:, b, 0:2]), r=[mk], w=[("xpfix", f)])
                        if not isval:
                            A(lambda e, acc=acc: e.activation(acc, acc, AF.Gelu), r=[ak], w=[ak])
                        else:
                            fa = sp_ * 4 + j
                            G(lambda e, acc=acc, j=j, fa=fa: e.tensor_tensor(act[:, fa, :], gacc[:, j, :], acc, ALU.mult),
                              r=[ak, ("gacc", j)], w=[("act", fa)])

        def downproj(blk):
            for dg in range(4):
                for rg in range(3):
                    i = load_slab(wslab(w_down, rg * 2048, dg * 512))
                    sk = f"slab{i}"
                    for fl in range(16):
                        f = rg * 16 + fl
                        for t in range(4):
                            T(lambda e, f=f, fl=fl, t=t, i=i: e.matmul(mm[:, t, :], lhsT=act[:, f, t * 128:(t + 1) * 128], rhs=slabs[i][:, fl, :],
                                                                        start=(f == 0), stop=(f == 47)),
                              r=[sk, ("act", f)], w=[f"mm{t}"], signal=(f == 47 or (fl == 15 and t == 3)))
                for t in range(4):
                    V(lambda e, t=t, dg=dg: e.tensor_tensor(resid[:, t, dg * 512:(dg + 1) * 512], mm[:, t, :], resid[:, t, dg * 512:(dg + 1) * 512], ALU.add),
                      r=[f"mm{t}", ("resid", t)], w=[("resid", t)])
            for t in range(4):
                A(lambda e, t=t: e.activation(gacc[:].rearrange("p a b -> p (a b)"), resid[:, t, :], AF.Square, accum_out=stat[:, 8 + t:9 + t]),
                  r=[("resid", t)], w=[("gacc", j) for j in range(4)] + [("st8", t)])
                A(lambda e, t=t: e.activation(stat[:, 12 + t:13 + t], stat[:, 8 + t:9 + t], AF.Sqrt, scale=1.0 / D, bias=EPS), r=[("st8", t)], w=[("st12", t)])
                V(lambda e, t=t: e.reciprocal(stat[:, 16 + t:17 + t], stat[:, 12 + t:13 + t]), r=[("st12", t)], w=[("st16", t)])
                V(lambda e, t=t: e.scalar_tensor_tensor(resid[:, t, :], resid[:, t, :], stat[:, 16 + t:17 + t], xs[:], ALU.mult, ALU.mult),
                  r=[("resid", t), ("st16", t), "xs"], w=[("resid", t)])
                r0 = blk * BT + t * 128
                S.dma("sp", yp[r0:r0 + 128, :], resid[:, t, :], reads=[("resid", t)], sem=f"yo{t}")

        def sample_block():
            NS = 64
            ALLACT = [("act", f) for f in range(48)]
            def bview(off, nbytes, dt=BF16):
                v = big[:, off // 2:(off + nbytes) // 2]
                return v if dt == BF16 else v.bitcast(F32)
            act_s = bview(0, 6336).rearrange("p (f t) -> p f t", t=66)
            xr_s = bview(6400, 3584, F32).rearrange("p (c t) -> p c t", t=112)
            gg_s = bview(10240, 2048, F32).rearrange("p (c t) -> p c t", t=64)
            qbd = bview(12288, 2048).rearrange("p (a s r) -> p a s r", a=2, s=16)
            ktc = bview(14336, 8448).rearrange("p (a s k) -> p a s k", a=2, s=16)
            vcb = bview(22784, 8192).rearrange("p (s d) -> p s d", d=256)
            vnew = bview(30976, 8192).rearrange("p (s d) -> p s d", d=256)
            s_sb = bview(39168, 1056, F32).rearrange("p (a k) -> p a k", a=2)
            p_sb = bview(40224, 528).rearrange("p (a k) -> p a k", a=2)
            ptc = bview(40752, 512).rearrange("p (a q) -> p a q", a=2)[:, :, 0:64]
            ptn = bview(41264, 512).rearrange("p (a q) -> p a q", a=2)[:, :, 0:64]
            o_g = [bview(41776 + i * 2048, 2048, F32).rearrange("p (a d) -> p a d", a=2) for i in range(2)]
            xc_a, ra_a, i_a, t_a, h_a, ysq_a = (rt[n][:].rearrange("p (c t) -> p c t", t=64) for n in ("xc", "ra", "i", "t", "hb", "ysq"))
            xcb_a = xcb[:].rearrange("p (c t) -> p c t", t=64)
            S.dma("sp", resid[0:NS, 0, :], xs_in[:, :], writes=[("resid", 0)], sem="x0")
            S.dma("sp", resid[64:66, 0, :], hfix_d[:, :], reads=["hfix_d"], writes=[("resid", 0)], sem="x0")
            S.dma("sp", xr_s[:, :, 0:48], src_conv_d.ap(), writes=["xr_s"] + ALLACT, sem="sl0")
            S.dma("sp", h_a[:, :, 0:16], srh_d.ap(), writes=["hb"], sem="sl0")
            S.dma("pool", ktc[:, :, :, 0:128], kcT_d.ap(), writes=["ktc"] + ALLACT, sem="sl1")
            S.dma("pool", vcb, vc_d.ap(), writes=["vcb"] + ALLACT, sem="sl2")
            S.dma("sp", sinkr[:], sinkr_d[:, :], writes=["sinkr"], sem="sl0")
            for t in range(4):
                S.dma("sp", bias_s[t:64:4, :], tr_d[:, 127 - t:127 - t + 132], reads=["tr_d"], writes=["bias_s"], sem="sl0")
            S.seal("sl0")
            S.dma("sp", o_sk[:, 0:124, :], kc_o[:, 4:128, :], sem="osk")
            S.dma("sp", o_sv[:, 0:124, :], vc_o[:, 4:128, :], sem="osv")
            V(lambda e: e.memset(qbd, 0.0), w=["qbd"] + ALLACT)
            norm_to_fm(resid[0:NS, 0, :], [("resid", 0)], C_GMIX, 0, dst_keyf=fmk, ntok=NS)
            fmr = lambda k: [fmk(k, 0)]
            for s_ in range(7):
                i = load_slab(wslab(w_in, 0, s_ * 512))
                sk = f"slab{i}"
                if s_ < 6:
                    for j in range(4):
                        b = next_mm()
                        c = (s_ % 2) * 4 + j
                        if s_ in (4, 5):
                            for k in range(16):
                                for a_ in range(2):
                                    hc = (a_ * 4 + j) * 64
                                    T(lambda e, b=b, k=k, a_=a_, hc=hc, i=i: e.matmul(mm[a_ * 64:(a_ + 1) * 64, b, 0:NS], lhsT=slabs[i][:, k, hc:hc + 64],
                                                                                       rhs=fm[:, k, 0:NS], start=(k == 0), stop=(k == 15)),
                                      r=[sk] + fmr(k), w=[f"mm{b}"], signal=(k == 15 and a_ == 1))
                            kp = s_ - 4
                            for kl in range(2):
                                ps = slice(kl * 64, (kl + 1) * 64)
                                dstq = qbd[ps, kp, :, kl * 16 + j * 4:kl * 16 + j * 4 + 4]
                                A(lambda e, b=b, ps=ps, dstq=dstq: e.activation(dstq, mm[ps, b, 0:NS].rearrange("p (t s) -> p s t", t=4), AF.Copy, scale=0.125),
                                  r=[f"mm{b}"], w=["qbd"])
                        else:
                            for k in range(16):
                                T(lambda e, b=b, k=k, j=j, i=i: e.matmul(mm[:, b, 0:NS], lhsT=slabs[i][:, k, j * 128:(j + 1) * 128], rhs=fm[:, k, 0:NS],
                                                                          start=(k == 0), stop=(k == 15)),
                                  r=[sk] + fmr(k), w=[f"mm{b}"], signal=(k == 15))
                            if s_ < 2:
                                A(lambda e, b=b, c=c: e.copy(xr_s[:, c, 48:112], mm[:, b, 0:NS]), r=[f"mm{b}"], w=["xr_s"])
                            else:
                                A(lambda e, b=b, c=c: e.activation(gg_s[:, c, :], mm[:, b, 0:NS], AF.Gelu), r=[f"mm{b}"], w=["gg_s"])
                    if s_ < 2:
                        b = next_mm()
                        for k in range(16):
                            T(lambda e, b=b, k=k, i=i: e.matmul(mm[0:NS, b, :], lhsT=fm[:, k, 0:NS], rhs=slabs[i][:, k, :], start=(k == 0), stop=(k == 15)),
                              r=[sk] + fmr(k), w=[f"mm{b}"], signal=(k == 15))
                        V(lambda e, b=b: e.tensor_copy(kvf[0:NS, :], mm[0:NS, b, :]), r=[f"mm{b}"], w=["kvf"])
                        S.dma("sp", o_src[:, s_ * 512:(s_ + 1) * 512], kvf[16:64, :], reads=["kvf"], sem="osrc")
                    if s_ == 3:
                        sample_rnn(xr_s, gg_s, xc_a, ra_a, i_a, t_a, h_a, ysq_a, xcb_a)
                else:
                    for kp in range(2):
                        b = next_mm()
                        for k in range(16):
                            T(lambda e, b=b, kp=kp, k=k, i=i: e.matmul(mm[:, b, 0:NS], lhsT=slabs[i][:, k, kp * 128:(kp + 1) * 128], rhs=fm[:, k, 0:NS],
                                                                        start=(k == 0), stop=(k == 15)),
                              r=[sk] + fmr(k), w=[f"mm{b}"], signal=(k == 15))
                        A(lambda e, b=b, kp=kp: e.copy(ktc[:, kp, :, 128:132], mm[:, b, 0:NS].rearrange("p (t s) -> p s t", t=4)), r=[f"mm{b}"], w=["ktc"])
                    b = next_mm()
                    for k in range(16):
                        T(lambda e, b=b, k=k, i=i: e.matmul(mm[0:NS, b, :], lhsT=fm[:, k, 0:NS], rhs=slabs[i][:, k, :], start=(k == 0), stop=(k == 15)),
                          r=[sk] + fmr(k), w=[f"mm{b}"], signal=(k == 15))
                    V(lambda e, b=b: e.tensor_copy(kvf[0:NS, :], mm[0:NS, b, :]), r=[f"mm{b}"], w=["kvf"])
                    V(lambda e, b=b: e.tensor_copy(pbuf[0:NS, 0, :], mm[0:NS, b, 256:512]), r=[f"mm{b}"], w=[("pbuf", 0)])
                    S.dma("sp", vscr[:, :], pbuf[0:NS, 0, :], reads=[("pbuf", 0)], writes=["vscr"], sem="sl3")
                    S.dma("sp", vnew[0:4, :, :], vscr.ap().rearrange("(t s) d -> t s d", t=4), reads=["vscr"], writes=["vnew"] + ALLACT, sem="sl3")
                    for t in range(4):
                        S.dma("sp", o_sk[:, 124 + t, :], kvf[t * 16:(t + 1) * 16, 0:256], reads=["kvf"], sem="osk")
                        S.dma("sp", o_sv[:, 124 + t, :], kvf[t * 16:(t + 1) * 16, 256:512], reads=["kvf"], sem="osv")
            A(lambda e: e.activation(rr[:, 0:NS], mm[:, 3, 0:NS], AF.Sqrt, scale=1.0 / DR, bias=EPS), r=["mm3"], w=["rr"])
            V(lambda e: e.reciprocal(rr[:, 0:NS], rr[:, 0:NS]), r=["rr"], w=["rr"])
            for c in range(8):
                V(lambda e, c=c: e.scalar_tensor_tensor(fm[:, c, 0:NS], gg_s[:, c, :], cvec[:, C_GRNN + c:C_GRNN + c + 1], rr[:, 0:NS], ALU.mult, ALU.mult),
                  r=["gg_s", "rr", "cvec"], w=[fmk(c, 0)])
            for gi in range(8):
                bk = gi % 2
                for si in range(2):
                    sq = gi * 2 + si
                    for kp in range(2):
                        T(lambda e, si=si, sq=sq, kp=kp, bk=bk: e.matmul(sc[kp * 32:(kp + 1) * 32, bk, si * 132:(si + 1) * 132], lhsT=qbd[:, kp, sq, :],
                                                                         rhs=ktc[:, kp, sq, :], start=True, stop=True),
                          r=["qbd", "ktc"], w=[f"sc{bk}"])
                V(lambda e, bk=bk: e.tensor_tensor(s_sb[0:64], sc[0:64, bk, 0:264].rearrange("p (a k) -> p a k", a=2),
                                                   bias_s[:].unsqueeze(1).to_broadcast([64, 2, 132]), ALU.add),
                  r=[f"sc{bk}", "bias_s"], w=["s_sb"])
                V(lambda e: e.tensor_reduce(sst[:, 0:2], s_sb[0:64], AX.X, ALU.max), r=["s_sb"], w=["sst"])
                V(lambda e: e.tensor_scalar(sst[:, 0:2], sst[:, 0:2], sinkr[:, 0:1], None, ALU.max), r=["sst", "sinkr"], w=["sst"])
                V(lambda e: e.tensor_scalar(sst[:, 2:4], sst[:, 0:2], -1.0, None, ALU.mult), r=["sst"], w=["sst"])
                V(lambda e: e.tensor_scalar(sst[:, 4:6], sst[:, 0:2], -1.0, sinkr[:, 0:1], ALU.mult, ALU.add), r=["sst", "sinkr"], w=["sst"])
                A(lambda e: e.activation(sst[:, 4:6], sst[:, 4:6], AF.Exp), r=["sst"], w=["sst"])
                for si in range(2):
                    A(lambda e, si=si: e.activation(p_sb[0:64, si, :], s_sb[0:64, si, :], AF.Exp, bias=sst[:, 2 + si:3 + si], accum_out=sst[:, 6 + si:7 + si]),
                      r=["s_sb", "sst"], w=["p_sb", "sst"])
                V(lambda e: e.tensor_tensor(sst[:, 8:10], sst[:, 6:8], sst[:, 4:6], ALU.add), r=["sst"], w=["sst"])
                V(lambda e: e.reciprocal(sst[:, 8:10], sst[:, 8:10]), r=["sst"], w=["sst"])
                og = o_g[gi % 2]
                ogk = f"o_g{gi % 2}"
                for si in range(2):
                    sq = gi * 2 + si
                    tpx = mm[:, 2 + si, :].bitcast(BF16)
                    T(lambda e, si=si, tpx=tpx: e.transpose(tpx[:, 0:64], p_sb[0:64, si, 0:128], ident_b[0:64, 0:64]),
                      r=["p_sb", "ident_b"], w=[f"mm{2 + si}"], signal=False)
                    T(lambda e, si=si, tpx=tpx: e.transpose(tpx[0:4, 64:128], p_sb[0:64, si, 128:132], ident_b[0:64, 0:64]),
                      r=["p_sb", "ident_b"], w=[f"mm{2 + si}"])
                    V(lambda e, si=si, tpx=tpx: e.tensor_copy(ptc[:, si, :], tpx[:, 0:64]), r=[f"mm{2 + si}"], w=["ptc"])
                    V(lambda e, si=si, tpx=tpx: e.tensor_copy(ptn[0:4, si, :], tpx[0:4, 64:128]), r=[f"mm{2 + si}"], w=["ptn"])
                    T(lambda e, si=si, sq=sq: e.matmul(mm[0:64, si, 0:256], lhsT=ptc[:, si, :], rhs=vcb[:, sq, :], start=True, stop=False),
                      r=["ptc", "vcb"], w=[f"mm{si}"], signal=False)
                    T(lambda e, si=si, sq=sq: e.matmul(mm[0:64, si, 0:256], lhsT=ptn[0:4, si, :], rhs=vnew[0:4, sq, :], start=False, stop=True),
                      r=["ptn", "vnew"], w=[f"mm{si}"])
                    V(lambda e, si=si, og=og: e.tensor_scalar(og[0:64, si, :], mm[0:64, si, 0:256], sst[:, 8 + si:9 + si], None, ALU.mult),
                      r=[f"mm{si}", "sst"], w=[ogk])
                S.dma("sp", oscr[:, gi * 2:gi * 2 + 2, :], og[0:64], reads=[ogk], writes=[("oscr", gi)], sem=f"og{gi % 2}")
            yat = xs[0:NS, 0:1024]
            for t in range(4):
                for kv in range(4):
                    src = bass.AP(oscr, (16 * kv + t) * 4096 + kv * 64, [[256, 16], [4 * 4096, 4], [1, 64]])
                    S.dma("sp", xs[t * 16:(t + 1) * 16, kv * 256:(kv + 1) * 256].rearrange("p (g d) -> p g d", g=4), src,
                          reads=[("oscr", gi_) for gi_ in range(8)], writes=["xs"], sem="yat")
            S.seal("yat")
            dump("s_yat", yat, ["xs"])
            A(lambda e: e.activation(xs[0:NS, 1024:2048], yat, AF.Square, accum_out=stat[0:NS, 4:5]), r=["xs"], w=["xs2", "stat4"])
            A(lambda e: e.activation(stat[0:NS, 5:6], stat[0:NS, 4:5], AF.Sqrt, scale=1.0 / DR, bias=EPS), r=["stat4"], w=["stat5"])
            V(lambda e: e.reciprocal(stat[0:NS, 6:7], stat[0:NS, 5:6]), r=["stat5"], w=["stat6"])
            V(lambda e: e.tensor_scalar(yat, yat, stat[0:NS, 6:7], None, ALU.mult), r=["xs", "stat6"], w=["xs"])
            for grp in range(2):
                for c4 in range(4):
                    c = grp * 4 + c4
                    T(lambda e, c4=c4, c=c: e.transpose(tp[:, 0, c4 * 128:c4 * 128 + NS], xs[0:NS, c * 128:(c + 1) * 128], ident[0:NS, 0:NS]),
                      r=["xs", "ident"], w=["tp0"], signal=(c4 == 3))
                V(lambda e, grp=grp: e.tensor_tensor(
                    fm[:, 8 + grp * 4:8 + (grp + 1) * 4, 0:NS],
                    tp[:, 0, :].rearrange("p (c t) -> p c t", t=128)[:, :, 0:NS],
                    cvec[:, C_GATT + grp * 4:C_GATT + grp * 4 + 4].unsqueeze(2).to_broadcast([128, 4, NS]), ALU.mult),
                  r=["tp0", "cvec"], w=[fmk(8 + grp * 4 + c4, 0) for c4 in range(4)])
            dump("s_merged", fm[:, :, 0:NS], [fmk(c, 0) for c in range(16)])
            dump("s_qbd", qbd, ["qbd"])
            dump("s_ktc", ktc, ["ktc"])
            dump("s_bias", bias_s[:], ["bias_s"])
            dump("s_vnew", vnew[0:4], ["vnew"])
            for dg in range(4):
                i = load_slab(wslab(w_out, 0, dg * 512))
                sk = f"slab{i}"
                b = next_mm(4)
                for e_ in range(16):
                    T(lambda e, e_=e_, i=i, b=b: e.matmul(mm[0:NS, b, :], lhsT=fm[:, e_, 0:NS], rhs=slabs[i][:, e_, :], start=(e_ == 0), stop=(e_ == 15)),
                      r=[sk, fmk(e_, 0)], w=[f"mm{b}"], signal=(e_ == 15))
                V(lambda e, b=b, dg=dg: e.tensor_tensor(resid[0:NS, 0, dg * 512:(dg + 1) * 512], mm[0:NS, b, :], resid[0:NS, 0, dg * 512:(dg + 1) * 512], ALU.add),
                  r=[f"mm{b}", ("resid", 0)], w=[("resid", 0)])
            dump("s_h", resid[0:NS, 0, :], [("resid", 0)])
            norm_to_fm(resid[0:NS, 0, :], [("resid", 0)], C_GFFN, 0, dst_keyf=fmk, ntok=NS)
            V(lambda e: e.memset(xpfix[:, :, 0:2], 0.0), w=[("xpfix", f) for f in range(96)])
            for r_ in range(NCORES):
                S.dma("sp", exh[r_ % 2][:], ex2_out[r_ * 128:(r_ + 1) * 128, :], reads=["ex2_out"], writes=[f"exh{r_ % 2}"], sem=f"exh{r_ % 2}")
                V(lambda e, r_=r_: e.scalar_tensor_tensor(xpfix[:, :, 0:2], exh[r_ % 2][:].rearrange("p (f t) -> p f t", t=2), cflag[:, 10 + r_:11 + r_],
                                                          xpfix[:, :, 0:2], ALU.mult, ALU.add),
                  r=[f"exh{r_ % 2}", "cflag"], w=[("xpfix", f) for f in range(96)])
            for sp_ in range(12):
                for isval in range(2):
                    i = load_slab(wslab(w_up, 0, isval * 6144 + sp_ * 512))
                    sk = f"slab{i}"
                    for j in range(4):
                        f = isval * 48 + sp_ * 4 + j
                        b = next_mm(4)
                        for k in range(16):
                            T(lambda e, b=b, k=k, j=j, i=i: e.matmul(mm[:, b, 0:NS], lhsT=slabs[i][:, k, j * 128:(j + 1) * 128], rhs=fm[:, k, 0:NS],
                                                                      start=(k == 0), stop=(k == 15)),
                              r=[sk, fmk(k, 0)], w=[f"mm{b}"], signal=(k == 15))
                        xp_ = xpf[f % 2]
                        xk = f"xpf{f % 2}"
                        S.dma("sp", xp_[:, 0:32], sfc_d[:, f, :], writes=[xk], sem=f"sfc{f % 2}")
                        A(lambda e, b=b, xp_=xp_: e.copy(xp_[:, 32:96], mm[:, b, 0:NS]), r=[f"mm{b}"], w=[xk])
                        acc = (accs[2][:] if isval else accs[j % 2][:])
                        ak = "accs2" if isval else f"accs{j % 2}"
                        if not isval:
                            acc = gacc[:, j, 0:66]
                            ak = ("gacc", j)
                        fw = lambda k, f=f: cvec[:, C_FW + k * 96 + f:C_FW + k * 96 + f + 1]
                        fb = cvec[:, C_FB + f:C_FB + f + 1]
                        A(lambda e, acc=acc, xp_=xp_, fw=fw, fb=fb: e.activation(acc[:, 0:64], xp_[:, 32:96], AF.Identity, scale=fw(2), bias=fb), r=[xk, "cvec"], w=[ak])
                        V(lambda e, acc=acc, xp_=xp_, fw=fw: e.scalar_tensor_tensor(acc[:, 0:64], xp_[:, 16:80], fw(1), acc[:, 0:64], ALU.mult, ALU.add), r=[xk, ak], w=[ak])
                        V(lambda e, acc=acc, xp_=xp_, fw=fw: e.scalar_tensor_tensor(acc[:, 0:64], xp_[:, 0:64], fw(0), acc[:, 0:64], ALU.mult, ALU.add), r=[xk, ak], w=[ak])
                        A(lambda e, acc=acc, f=f, fw=fw, fb=fb: e.activation(acc[:, 64:66], xpfix[:, f, 2:4], AF.Identity, scale=fw(2), bias=fb), r=[("xpfix", f), "cvec"], w=[ak])
                        V(lambda e, acc=acc, f=f, fw=fw: e.scalar_tensor_tensor(acc[:, 64:66], xpfix[:, f, 1:3], fw(1), acc[:, 64:66], ALU.mult, ALU.add), r=[("xpfix", f), ak], w=[ak])
                        V(lambda e, acc=acc, f=f, fw=fw: e.scalar_tensor_tensor(acc[:, 64:66], xpfix[:, f, 0:2], fw(0), acc[:, 64:66], ALU.mult, ALU.add), r=[("xpfix", f), ak], w=[ak])
                        q4 = f % 4
                        tb_ = (f // 4) % 2
                        T(lambda e, xp_=xp_, q4=q4, tb_=tb_: e.transpose(tp[0:32, tb_, q4 * 128:(q4 + 1) * 128], xp_[:, 64:96], ident[:]),
                          r=[xk, "ident"], w=[f"tp{tb_}"])
                        if q4 == 3:
                            V(lambda e, tb_=tb_: e.tensor_copy(kvf[0:32, :], tp[0:32, tb_, :]), r=[f"tp{tb_}"], w=["kvf"])
                            f0 = f - 3
                            S.dma("sp", o_sfc[:, f0 * 128:f0 * 128 + 512], kvf[0:32, :], reads=["kvf"], sem="osfc")
                        if not isval:
                            A(lambda e, acc=acc: e.activation(acc, acc, AF.Gelu), r=[ak], w=[ak])
                        else:
                            fa = sp_ * 4 + j
                            G(lambda e, acc=acc, j=j, fa=fa: e.tensor_tensor(act_s[:, fa, :], gacc[:, j, 0:66], acc, ALU.mult),
                              r=[ak, ("gacc", j)], w=[("act", fa)])
            dump("s_act", act_s, [("act", f) for f in range(48)])
            S.dma("sp", xs[:], gfin_d[0:1, :].partition_broadcast(128), writes=["xs"], sem="gfin")
            for dg in range(4):
                b = next_mm(4)
                for rg in range(3):
                    i = load_slab(wslab(w_down, rg * 2048, dg * 512))
                    sk = f"slab{i}"
                    for fl in range(16):
                        f = rg * 16 + fl
                        T(lambda e, f=f, fl=fl, i=i, b=b: e.matmul(mm[0:66, b, :], lhsT=act_s[:, f, :], rhs=slabs[i][:, fl, :], start=(f == 0), stop=(f == 47)),
                          r=[sk, ("act", f)], w=[f"mm{b}"], signal=(fl == 15))
                V(lambda e, b=b, dg=dg: e.tensor_tensor(resid[0:66, 0, dg * 512:(dg + 1) * 512], mm[0:66, b, :], resid[0:66, 0, dg * 512:(dg + 1) * 512], ALU.add),
                  r=[f"mm{b}", ("resid", 0)], w=[("resid", 0)])
            A(lambda e: e.activation(gacc[0:66].rearrange("p a b -> p (a b)"), resid[0:66, 0, :], AF.Square, accum_out=stat[0:66, 8:9]),
              r=[("resid", 0)], w=[("gacc", j) for j in range(4)] + [("st8", 0)])
            A(lambda e: e.activation(stat[0:66, 12:13], stat[0:66, 8:9], AF.Sqrt, scale=1.0 / D, bias=EPS), r=[("st8", 0)], w=[("st12", 0)])
            V(lambda e: e.reciprocal(stat[0:66, 16:17], stat[0:66, 12:13]), r=[("st12", 0)], w=[("st16", 0)])
            V(lambda e: e.scalar_tensor_tensor(resid[0:66, 0, :], resid[0:66, 0, :], stat[0:66, 16:17], xs[0:66, :], ALU.mult, ALU.mult),
              r=[("resid", 0), ("st16", 0), "xs"], w=[("resid", 0)])
            S.dma("sp", ys_o[:, :], resid[0:66, 0, :], reads=[("resid", 0)], sem="yo0")

        def sample_rnn(xr_s, gg_s, xc_a, ra_a, i_a, t_a, h_a, ysq_a, xcb_a):
            for c in range(8):
                cw = lambda k, c=c: cvec[:, C_CW + k * 8 + c:C_CW + k * 8 + c + 1]
                A(lambda e, c=c, cw=cw: e.activation(xc_a[:, c, :], xr_s[:, c, 48:112], AF.Identity, scale=cw(3), bias=cvec[:, C_CB + c:C_CB + c + 1]),
                  r=["xr_s", "cvec"], w=["xc"])
                for k in range(3):
                    V(lambda e, c=c, cw=cw, k=k: e.scalar_tensor_tensor(xc_a[:, c, :], xr_s[:, c, k * 16:k * 16 + 64], cw(k), xc_a[:, c, :], ALU.mult, ALU.add),
                      r=["xr_s", "xc"], w=["xc"])
            G(lambda e: e.tensor_copy(xcb[:], rt["xc"][:]), r=["xc"], w=["xcb"])
            for c in range(8):
                T(lambda e, c=c: e.matmul(sc[:, 0, 0:64], lhsT=wbd[0][:, c, :], rhs=xcb_a[:, c, :], start=True, stop=True), r=["xcb", "wbd0"], w=["sc0"])
                T(lambda e, c=c: e.matmul(sc[:, 1, 0:64], lhsT=wbd[1][:, c, :], rhs=xcb_a[:, c, :], start=True, stop=True), r=["xcb", "wbd1"], w=["sc1"])
                A(lambda e, c=c: e.activation(ra_a[:, c, :], sc[:, 0, 0:64], AF.Sigmoid, bias=cvec[:, C_BA + c:C_BA + c + 1]), r=["sc0", "cvec"], w=["ra"])
                A(lambda e, c=c: e.activation(i_a[:, c, :], sc[:, 1, 0:64], AF.Sigmoid, bias=cvec[:, C_BX + c:C_BX + c + 1]), r=["sc1", "cvec"], w=["ib"])
                A(lambda e, c=c: e.activation(ra_a[:, c, :], ra_a[:, c, :], AF.Exp, scale=cneg[:, c:c + 1]), r=["ra", "cneg"], w=["ra"])
            ra, ib, tb, xc = rt["ra"], rt["i"], rt["t"], rt["xc"]
            V(lambda e: e.tensor_tensor(tb[:], ra[:], ra[:], ALU.mult), r=["ra"], w=["tb"])
            A(lambda e: e.activation(tb[:], tb[:], AF.Sqrt, scale=-1.0, bias=1.0), r=["tb"], w=["tb"])
            G(lambda e: e.tensor_tensor(ib[:], ib[:], xc[:], ALU.mult), r=["ib", "xc"], w=["ib"])
            V(lambda e: e.tensor_tensor(ib[:], ib[:], tb[:], ALU.mult), r=["ib", "tb"], w=["ib"])
            hs = t_a
            for t in range(4):
                prev = h_a[:, :, 0:16] if t == 0 else hs[:, :, (t - 1) * 16:t * 16]
                V(lambda e, t=t, prev=prev: e.tensor_tensor(hs[:, :, t * 16:(t + 1) * 16], ra_a[:, :, t * 16:(t + 1) * 16], prev, ALU.mult),
                  r=["ra", "hb", "tb"], w=["tb"])
                V(lambda e, t=t: e.tensor_tensor(hs[:, :, t * 16:(t + 1) * 16], hs[:, :, t * 16:(t + 1) * 16], i_a[:, :, t * 16:(t + 1) * 16], ALU.add),
                  r=["ib", "tb"], w=["tb"])
            S.dma("sp", o_srh.ap(), hs[:, :, 48:64], reads=["tb"], sem="osrh")
            G(lambda e: e.tensor_tensor(gg_s, gg_s, hs, ALU.mult), r=["gg_s", "tb"], w=["gg_s"])
            G(lambda e: e.tensor_tensor(ysq_a, gg_s, gg_s, ALU.mult), r=["gg_s"], w=["ysq"])
            for c in range(8):
                T(lambda e, c=c: e.matmul(mm[:, 3, 0:64], lhsT=ones_f[:], rhs=ysq_a[:, c, :], start=(c == 0), stop=(c == 7)),
                  r=["ysq", "ones"], w=["mm3"], signal=(c == 7))

        ident_b = sb("ident_b", [128, 128], BF16)
        V(lambda e: e.tensor_copy(ident_b[:], ident[:]), r=["ident"], w=["ident_b"])

        if dbg.get("prepass", True):
            V(lambda e: e.memset(rtot[:], 0.0), w=["rtot"])
            for blk in range(NBLK):
                prompt_block(blk, pre=True)
            V(lambda e: e.tensor_tensor(exs[:, 0:8], rtot[:], cneg[:], ALU.mult), r=["rtot", "cneg"], w=["exs"])
            A(lambda e: e.activation(exs[:, 0:8], exs[:, 0:8], AF.Exp), r=["exs"], w=["exs"])
            V(lambda e: e.tensor_copy(exs[:, 8:16], hstate[:]), r=["hstate"], w=["exs"])
            S.dma("sp", ex_in[:, :], exs[:], reads=["exs"], writes=["ex_in"], sem="ex")
            S.custom("pool", lambda e: e.collective_compute("AllGather", ALU.bypass, replica_groups=[list(range(NCORES))],
                                                            ins=[ex_in.ap().opt()], outs=[ex_out.ap().opt()]),
                     reads=["ex_in"], writes=["ex_out"], sem="cc1", inc=1)
            S.dma("sp", exg[:], ex_out.ap().rearrange("(r p) c -> p r c", p=128), reads=["ex_out"], writes=["exg"], sem="ex")
            V(lambda e: e.memset(hstate[:], 0.0), r=["exs"], w=["hstate"])
            for r_ in range(NCORES):
                mcol = cflag[:, 1 + r_:2 + r_]
                V(lambda e, r_=r_, mcol=mcol: e.tensor_scalar(exa[:], exg[:, r_, 0:8], -1.0, mcol, ALU.add, ALU.mult), r=["exg", "cflag"], w=["exa"])
                V(lambda e: e.tensor_scalar(exa[:], exa[:], 1.0, None, ALU.add), r=["exa"], w=["exa"])
                V(lambda e: e.tensor_tensor(hstate[:], hstate[:], exa[:], ALU.mult), r=["exa", "hstate"], w=["hstate"])
                V(lambda e, r_=r_, mcol=mcol: e.scalar_tensor_tensor(hstate[:], exg[:, r_, 8:16], mcol, hstate[:], ALU.mult, ALU.add),
                  r=["exg", "cflag", "hstate"], w=["hstate"])
        if upto != 'const':
            for blk in range(nblk):
                prompt_block(blk)

        if dbg.get("sample", True) and upto == 'all' and nblk == NBLK:
            sample_block()
        S.dma("sp", o_prh[:, :], hstate[:], reads=["hstate"], sem="o_prh")
        for c in range(8):
            T(lambda e, c=c: e.transpose(tp[0:3, c // 4, (c % 4) * 128:(c % 4 + 1) * 128], xrt[:, c, :], ident[:]),
              r=["xrt", "ident"], w=[f"tp{c // 4}"], signal=(c % 4 == 3))
        V(lambda e: e.tensor_copy(small_o[0:3, 0:1024].rearrange("p (a b) -> p a b", a=2), tp[0:3, :, :]), r=["tp0", "tp1"], w=["xs"])
        S.dma("sp", o_prc[:, :], small_o[0:3, 0:1024], reads=["xs"], sem="o_prc")
        for tk in range(2):
            T(lambda e, tk=tk: e.transpose(tp[0:96, tk, 0:128], hal[:, :, tk], ident[:]), r=[("hal", f) for f in range(96)] + ["ident"], w=[f"tp{tk}"])
            V(lambda e, tk=tk: e.tensor_copy(kvf[0:96, tk * 128:(tk + 1) * 128], tp[0:96, tk, 0:128]), r=[f"tp{tk}"], w=["kvf"])
            S.dma("sp", o_pfc[tk:tk + 1, :].rearrange("o (c p) -> (o c) p", p=128), kvf[0:96, tk * 128:(tk + 1) * 128], reads=["kvf"], sem=f"o_pfc{tk}")


        S.wait_all("sp")
        S.emit()
        build.last_sched = S
        build.sbuf_left = nc.sbuf_bytes_remaining
    return nc


def _t5_bucket_np(d):
    n = np.maximum(d, 0)
    nf = np.maximum(n, 1).astype(np.float32)
    large = 16 + (np.log(nf / 16) / math.log(128 / 16) * 16).astype(np.int32)
    large = np.minimum(large, 31)
    return np.where(n < 16, n, large)


def _feat(v):
    v = np.asarray(v, np.float32).reshape(-1, 128)
    return np.ascontiguousarray(v.T)


_NC_CACHE = {}


def kernel(x_prompt, x_sample, state_rnn_conv, state_rnn_h, cache_win_k, cache_win_v, state_ffn_conv,
           norm_mix_g, w_in, rnn_conv_w, rnn_conv_b, w_gate_a, b_gate_a, w_gate_x, b_gate_x, rnn_lambda,
           attn_sinks, rel_bias_table, gn_rnn_g, gn_attn_g, w_out, norm_ffn_g, w_up, ffn_conv_w, ffn_conv_b,
           w_down, norm_final_g):
    f32 = lambda a: np.ascontiguousarray(np.asarray(a, dtype=np.float32))
    x_prompt = f32(x_prompt)
    x_sample, state_rnn_conv, state_rnn_h = f32(x_sample), f32(state_rnn_conv), f32(state_rnn_h)
    cache_win_k, cache_win_v, state_ffn_conv = f32(cache_win_k), f32(cache_win_v), f32(state_ffn_conv)
    cv = np.zeros((128, NCV), np.float32)
    cv[:, C_GMIX:C_GMIX + 16] = _feat(norm_mix_g[0])
    cv[:, C_GFFN:C_GFFN + 16] = _feat(norm_ffn_g[0])
    cv[:, C_GRNN:C_GRNN + 8] = _feat(gn_rnn_g[0])
    cv[:, C_GATT:C_GATT + 8] = _feat(gn_attn_g[0])
    for k in range(4):
        cv[:, C_CW + 8 * k:C_CW + 8 * k + 8] = _feat(np.asarray(rnn_conv_w)[0, k])
    cv[:, C_CB:C_CB + 8] = _feat(rnn_conv_b[0])
    cv[:, C_BA:C_BA + 8] = _feat(b_gate_a[0])
    cv[:, C_BX:C_BX + 8] = _feat(b_gate_x[0])
    cv[:, C_LAM:C_LAM + 8] = _feat(rnn_lambda[0])
    for k in range(3):
        cv[:, C_FW + 96 * k:C_FW + 96 * k + 96] = _feat(np.asarray(ffn_conv_w)[0, k])
    cv[:, C_FB:C_FB + 96] = _feat(ffn_conv_b[0])
    j = np.arange(384)
    dist = 255 - j
    valid = (dist >= 0) & (dist < 128)
    oh2 = np.zeros((32, 384), np.float32)
    oh2[_t5_bucket_np(dist)[valid], j[valid]] = 1.0
    mrow = np.tile(np.where(valid, 0.0, NEG).astype(np.float32)[None, :], (16, 1))
    ident = np.eye(128, dtype=np.float32)
    common = {
        "w_in": f32(w_in[0]), "w_out": f32(w_out[0]), "w_up": f32(w_up[0]), "w_down": f32(w_down[0]),
        "cvec": cv, "gfin": f32(norm_final_g).reshape(1, D), "sinks": f32(attn_sinks).reshape(1, 16),
        "table": f32(rel_bias_table), "wga": f32(w_gate_a[0]), "wgx": f32(w_gate_x[0]),
        "ident": ident, "oh2": oh2, "mrow": mrow, "aident": np.ascontiguousarray(ident[::-1]),
    }
    in_maps = []
    for c in range(NCORES):
        s, jj = c // 4, c % 4
        xpc = np.zeros((128 + NT, D), np.float32)
        t0 = jj * NT
        xpc[128:] = x_prompt[s, t0:t0 + NT]
        if jj > 0:
            xpc[:128] = x_prompt[s, t0 - 128:t0]
        cf = np.zeros((128, 32), np.float32)
        cf[:, 0] = 1.0 if jj > 0 else 0.0
        for r in range(NCORES):
            cf[:, 1 + r] = 1.0 if (r // 4 == s and r < c) else 0.0
            cf[:, 10 + r] = 1.0 if (r // 4 == s and r == c - 1) else 0.0
        cf[:, 9] = 0.0 if jj > 0 else NEG
        m = dict(common)
        m["xp"] = xpc
        m["cflag"] = cf
        sl = slice(16 * c, 16 * c + 16)
        m["xs_in"] = np.ascontiguousarray(x_sample[sl].transpose(1, 0, 2).reshape(64, D))
        m["src_conv"] = np.ascontiguousarray(state_rnn_conv[0, sl].reshape(16, 3, 8, 128).transpose(3, 2, 1, 0).reshape(128, 8, 48))
        m["srh"] = np.ascontiguousarray(state_rnn_h[0, sl].reshape(16, 8, 128).transpose(2, 1, 0))
        kc = cache_win_k[0, sl].reshape(16, 128, 256)
        m["kcT"] = np.ascontiguousarray(kc.reshape(16, 128, 2, 128).transpose(3, 2, 0, 1))
        m["vc"] = np.ascontiguousarray(cache_win_v[0, sl].reshape(16, 128, 256).transpose(1, 0, 2))
        m["kc_o"] = np.ascontiguousarray(kc)
        m["vc_o"] = np.ascontiguousarray(cache_win_v[0, sl].reshape(16, 128, 256))
        m["sfc"] = np.ascontiguousarray(state_ffn_conv[0, sl].reshape(16, 2, 96, 128).transpose(3, 2, 1, 0).reshape(128, 96, 32))
        m["sinkr"] = np.ascontiguousarray(np.repeat(f32(attn_sinks).reshape(16), 4).reshape(64, 1))
        in_maps.append(m)
    if "nc" not in _NC_CACHE:
        _NC_CACHE["nc"] = build()
    if "keep" not in _NC_CACHE:
        _NC_CACHE["keep"] = jax.device_put(np.zeros((8,), np.float32))
    res = run_bass_kernel_spmd(_NC_CACHE["nc"], in_maps, core_ids=list(range(NCORES)))
    R = res.results
    y_prompt = np.zeros((2, 8192, D), np.float32)
    for c in range(NCORES):
        s, jj = c // 4, c % 4
        y_prompt[s, jj * NT:(jj + 1) * NT] = R[c]["yp"]
    lastc = [3, 7]
    p_rnn_conv = np.stack([R[c]["o_prc"] for c in lastc])[None]
    p_rnn_h = np.stack([np.ascontiguousarray(R[c]["o_prh"].T).reshape(-1) for c in lastc])[None]
    p_win_k = np.stack([R[c]["o_pk"].reshape(128, 4, 64) for c in lastc])[None]
    p_win_v = np.stack([R[c]["o_pv"].reshape(128, 4, 64) for c in lastc])[None]
    p_ffn = np.stack([R[c]["o_pfc"] for c in lastc])[None]
    _NC_CACHE["last"] = R
    y_sample = np.zeros((128, 4, D), np.float32)
    s_rnn_conv = np.zeros((1, 128, 3, DR), np.float32)
    s_rnn_h = np.zeros((1, 128, DR), np.float32)
    s_win_k = np.zeros((1, 128, 128, 4, 64), np.float32)
    s_win_v = np.zeros((1, 128, 128, 4, 64), np.float32)
    s_ffn = np.zeros((1, 128, 2, 12288), np.float32)
    for c in range(NCORES):
        sl = slice(16 * c, 16 * c + 16)
        ys = R[c]["ys_o"]
        y_sample[sl] = ys[0:64].reshape(4, 16, D).transpose(1, 0, 2)
        y_prompt[c // 4, (c % 4) * NT:(c % 4) * NT + 2] = ys[64:66]
        s_rnn_conv[0, sl] = R[c]["o_src"].reshape(3, 16, DR).transpose(1, 0, 2)
        s_rnn_h[0, sl] = R[c]["o_srh"].transpose(2, 1, 0).reshape(16, DR)
        s_win_k[0, sl] = R[c]["o_sk"].reshape(16, 128, 4, 64)
        s_win_v[0, sl] = R[c]["o_sv"].reshape(16, 128, 4, 64)
        s_ffn[0, sl] = R[c]["o_sfc"].reshape(2, 16, 12288).transpose(1, 0, 2)
    return (y_prompt, y_sample, p_rnn_conv, p_rnn_h, p_win_k, p_win_v, p_ffn,
            s_rnn_conv, s_rnn_h, s_win_k, s_win_v, s_ffn)
```

```python
import contextlib
import math
import numpy as np
import concourse.bass as bass
import concourse.mybir as mybir
from concourse.bass_utils import run_bass_kernel_spmd

F32 = mybir.dt.float32
BF16 = mybir.dt.bfloat16
AF = mybir.ActivationFunctionType
ALU = mybir.AluOpType
AX = mybir.AxisListType

NCORES = 8
D = 2048
DR = 1024
NT = 2048
NBLK = 4
BT = 512
EPS = 1e-6
NEG = -1e30
EPOCH = 12000

C_GMIX, C_GFFN, C_GRNN, C_GATT = 0, 16, 32, 40
C_CW, C_CB, C_BA, C_BX, C_LAM = 48, 80, 88, 96, 104
C_FW, C_FB = 112, 400
NCV = 496


class Tok:
    __slots__ = ("sem", "val", "eng")

    def __init__(self, sem, val, eng):
        self.sem, self.val, self.eng = sem, val, eng


class Sched:
    def __init__(self, nc, stack):
        self.nc = nc
        self.stack = stack
        self.engs = ("pe", "act", "dve", "pool", "sp")
        self.q = {e: [] for e in self.engs}
        self.cnt = {e: 0 for e in self.engs}
        self.esem = {e: None for e in self.engs}
        self.nsem = 0
        self.waited = {e: {} for e in self.engs}
        self.last_w = {}
        self.readers = {}
        self.dsems = {}
        self.pending = {e: [] for e in self.engs}
        self.dtoks = {}

    def new_sem(self, name):
        self.nsem += 1
        return self.stack.enter_context(self.nc.semaphore(f"{name}_{self.nsem}"))

    def _eng_sem(self, e):
        if self.esem[e] is None or self.cnt[e] >= EPOCH:
            self.esem[e] = self.new_sem("e" + e)
            self.cnt[e] = 0
        return self.esem[e]

    def _deps(self, eng, reads, writes):
        toks = []
        for k in reads:
            t = self.last_w.get(k)
            if t is not None:
                toks.append(t)
        for k in writes:
            t = self.last_w.get(k)
            if t is not None:
                toks.append(t)
            toks.extend(self.readers.get(k, ()))
        waits = {}
        for t in toks:
            if t.eng == eng and eng == "pe":
                continue
            assert t.sem is not None, "dependency on unsignaled op"
            key = id(t.sem)
            if self.waited[eng].get(key, 0) >= t.val:
                continue
            if key not in waits or waits[key][1] < t.val:
                waits[key] = (t.sem, t.val)
        for key, (s, v) in waits.items():
            self.waited[eng][key] = v
        return list(waits.values())

    def _commit(self, tok, reads, writes):
        for k in writes:
            self.last_w[k] = tok
            self.readers[k] = []
        for k in reads:
            self.readers.setdefault(k, []).append(tok)

    @staticmethod
    def _split(reads, writes):
        ps = [k for k in reads if isinstance(k, str) and k[:2] in ("mm", "sc", "tp")]
        if ps:
            reads = [k for k in reads if k not in ps]
            writes = list(writes) + ps
        return reads, writes

    def op(self, eng, fn, reads=(), writes=(), signal=True):
        reads, writes = self._split(reads, writes)
        waits = self._deps(eng, reads, writes)
        if signal:
            sem = self._eng_sem(eng)
            self.cnt[eng] += 1
            tok = Tok(sem, self.cnt[eng], eng)
            for p in self.pending[eng]:
                p.sem, p.val = sem, tok.val
            self.pending[eng] = []
        else:
            tok = Tok(None, None, eng)
            self.pending[eng].append(tok)
        self._commit(tok, reads, writes)
        self.q[eng].append((waits, fn, (tok.sem, 1) if signal else None))
        return tok

    def dma(self, qeng, out, in_, reads=(), writes=(), sem=None, **kw):
        waits = self._deps(qeng, reads, writes)
        if sem not in self.dsems:
            self.dsems[sem] = [self.new_sem("d"), 0]
        ds = self.dsems[sem]
        ds[1] += 16
        tok = Tok(ds[0], ds[1], "dma")
        self.dtoks.setdefault(sem, []).append(tok)
        self._commit(tok, reads, writes)
        self.q[qeng].append((waits, lambda e: e.dma_start(out=out, in_=in_, **kw), (ds[0], 16)))
        return tok

    def seal(self, sem):
        ds = self.dsems[sem]
        for t in self.dtoks.get(sem, ()):
            t.val = ds[1]

    def custom(self, qeng, fn, reads=(), writes=(), sem=None, inc=1):
        waits = self._deps(qeng, reads, writes)
        if sem not in self.dsems:
            self.dsems[sem] = [self.new_sem("c"), 0]
        ds = self.dsems[sem]
        ds[1] += inc
        tok = Tok(ds[0], ds[1], "dma")
        self._commit(tok, reads, writes)
        self.q[qeng].append((waits, fn, (ds[0], inc)))
        return tok

    def wait_all(self, eng="sp"):
        waits = []
        for name, (s, c) in self.dsems.items():
            if self.waited[eng].get(id(s), 0) < c:
                waits.append((s, c))
                self.waited[eng][id(s)] = c
        self.q[eng].append((waits, None, None))

    def emit(self):
        assert all(not self.pending[e] for e in self.engs), "unsignaled trailing ops"
        nc = self.nc
        with nc.Block() as block:
            def mk(ename):
                def body(e):
                    for waits, fn, inc in self.q[ename]:
                        for s, v in waits:
                            e.wait_ge(s, v)
                        if fn is not None:
                            ins = fn(e)
                            if inc is not None:
                                ins.then_inc(inc[0], inc[1])
                return body
            block.tensor(mk("pe"))
            block.scalar(mk("act"))
            block.vector(mk("dve"))
            block.gpsimd(mk("pool"))
            block.sync(mk("sp"))


def build(dbg=None, nblk=NBLK, upto='all'):
    nc = bass.Bass("TRN2", target_bir_lowering=False)
    dbg = dbg or {}

    def din(name, shape, dt=F32):
        return nc.dram_tensor(name, list(shape), dt, kind="ExternalInput")

    def dout(name, shape, dt=F32):
        return nc.dram_tensor(name, list(shape), dt, kind="ExternalOutput")

    xp = din("xp", [128 + NT, D])
    w_in = din("w_in", [D, 3584])
    w_out = din("w_out", [D, D])
    w_up = din("w_up", [D, 12288])
    w_down = din("w_down", [6144, D])
    cvec_d = din("cvec", [128, NCV])
    gfin_d = din("gfin", [1, D])
    sinks_d = din("sinks", [1, 16])
    table_d = din("table", [32, 16])
    wga_d = din("wga", [16, 64, 64])
    wgx_d = din("wgx", [16, 64, 64])
    ident_d = din("ident", [128, 128])
    oh2_d = din("oh2", [32, 384])
    mrow_d = din("mrow", [16, 384])
    cflag_d = din("cflag", [128, 32])
    aident_d = din("aident", [128, 128])
    xs_in = din("xs_in", [64, D])
    src_conv_d = din("src_conv", [128, 8, 48])
    srh_d = din("srh", [128, 8, 16])
    kcT_d = din("kcT", [128, 2, 16, 128])
    vc_d = din("vc", [128, 16, 256])
    kc_o = din("kc_o", [16, 128, 256])
    vc_o = din("vc_o", [16, 128, 256])
    sfc_d = din("sfc", [128, 96, 32])
    sinkr_d = din("sinkr", [64, 1])
    yp = dout("yp", [NT, D])
    ys_o = dout("ys_o", [66, D])
    o_src = dout("o_src", [48, DR])
    o_srh = dout("o_srh", [128, 8, 16])
    o_sk = dout("o_sk", [16, 128, 256])
    o_sv = dout("o_sv", [16, 128, 256])
    o_sfc = dout("o_sfc", [32, 12288])
    o_prc = dout("o_prc", [3, DR])
    o_prh = dout("o_prh", [128, 8])
    o_pk = dout("o_pk", [128, 256])
    o_pv = dout("o_pv", [128, 256])
    o_pfc = dout("o_pfc", [2, 12288])
    tr_d = nc.dram_tensor("tr_scratch", [16, 384], F32)
    ex_in = nc.dram_tensor("ex_in", [128, 16], F32)
    ex2_in = nc.dram_tensor("ex2_in", [128, 192], F32)
    ex2_out = nc.dram_tensor("ex2_out", [NCORES * 128, 192], F32)
    hfix_d = nc.dram_tensor("hfix_d", [2, D], F32)
    oscr = nc.dram_tensor("oscr", [64, 16, 256], F32)
    vscr = nc.dram_tensor("vscr", [64, 256], BF16)
    ex_out = nc.dram_tensor("ex_out", [NCORES * 128, 16], F32)
    dbg_out = {}

    with contextlib.ExitStack() as st:
        S = Sched(nc, st)

        def sb(name, shape, dt=F32):
            return st.enter_context(nc.sbuf_tensor("s_" + name, list(shape), dt))

        def A(fn, r=(), w=()):
            return S.op("act", fn, r, w)

        def V(fn, r=(), w=()):
            return S.op("dve", fn, r, w)

        def G(fn, r=(), w=()):
            return S.op("pool", fn, r, w)

        def T(fn, r=(), w=(), signal=True):
            return S.op("pe", fn, r, w, signal=signal)

        resid = sb("resid", [128, 4, D])
        fm = sb("fm", [128, 16, BT], BF16)
        big = sb("big", [128, 48 * BT], BF16)
        act = big[:].rearrange("p (f t) -> p f t", t=BT)
        xr = big[:, 0:8240].bitcast(F32).rearrange("p (c t) -> p c t", t=515)
        gg = big[:, 8448:8448 + 8192].bitcast(F32).rearrange("p (c t) -> p c t", t=BT)
        qT = big[:, 16640:16640 + 4096].rearrange("p (c t) -> p c t", t=BT)
        kT = sb("kT", [128, 2, 128 + BT], BF16)
        vtok = sb("vtok", [128, 5, 256], BF16)
        bias = sb("bias", [128, 16, 256])
        xs = sb("xs", [128, D])
        s_half = xs[:].rearrange("p (h t) -> p h t", t=256)
        pbuf = sb("pbuf", [128, 2, 256], BF16)
        ptb = sb("ptb", [128, 2, 2, 128], BF16)
        rt = {n: sb("rt_" + n, [128, BT]) for n in ("xc", "ra", "i", "t", "hb", "ysq")}
        xcb = sb("xcb", [128, BT], BF16)
        rr = sb("rr", [128, BT])
        gacc = sb("gacc", [128, 4, BT])
        vacc = sb("vacc", [128, 2, BT])
        hal = sb("hal", [128, 96, 2])
        slabs = [sb(f"slab{i}", [128, 16, 512], BF16) for i in range(2)]
        wbd = [sb(f"wbd{i}", [128, 8, 128], BF16) for i in range(2)]
        ident = sb("ident", [128, 128])
        ones_f = sb("ones_f", [128, 128])
        cvec = sb("cvec", [128, NCV])
        sinks_b = sb("sinks_b", [128, 16])
        cflag = sb("cflag", [128, 32])
        cneg = sb("cneg", [128, 8])
        hstate = sb("hstate", [128, 8])
        xrt = sb("xrt", [128, 8, 3])
        first2 = sb("first2", [128, 96, 2])
        xpfix = sb("xpfix", [128, 96, 4])
        exh = [sb(f"exh{i}", [128, 192]) for i in range(2)]
        bias_s = sb("bias_s", [64, 132])
        sinkr = sb("sinkr", [64, 1])
        sst = sb("sst", [64, 16])
        xpf = [sb(f"xpf{i}", [128, 96]) for i in range(2)]
        accs = [sb(f"accs{i}", [128, 66]) for i in range(3)]
        rsum = sb("rsum", [128, 8])
        rtot = sb("rtot", [128, 8])
        exs = sb("exs", [128, 16])
        exg = sb("exg", [128, 8, 16])
        exa = sb("exa", [128, 8])
        kvf = sb("kvf", [128, 512])
        stat = sb("stat", [128, 64])
        m8 = sb("m8", [128, 16])
        negm = sb("negm", [128, 16])
        rs = sb("rs", [128, 16])
        rden = sb("rden", [128, 16])
        es = sb("es", [128, 16])
        tab_sb = sb("tab_sb", [32, 16])
        oh2_sb = gacc[0:32, 0, 0:384]
        tr_sb = gacc[0:16, 1, 0:384]
        mrow_sb = gacc[0:16, 2, 0:384]
        small_o = xs
        aident = sb("aident", [128, 128])

        mm = st.enter_context(nc.psum_tensor("mm", [128, 4, 512], F32))
        sc = st.enter_context(nc.psum_tensor("sc", [128, 2, 512], F32))
        tp = st.enter_context(nc.psum_tensor("tp", [128, 2, 512], F32))
        tpb = tp[:, 1, :].bitcast(BF16)

        cq = "sp"
        S.dma(cq, ident[:], ident_d[:, :], writes=["ident"], sem="const")
        S.dma(cq, cvec[:], cvec_d[:, :], writes=["cvec"], sem="const")
        S.dma(cq, sinks_b[:], sinks_d[0:1, :].partition_broadcast(128), writes=["sinks_b"], sem="const")
        S.dma(cq, cflag[:], cflag_d[:, :], writes=["cflag"], sem="const")
        S.dma(cq, tab_sb[:], table_d[:, :], writes=["tab"], sem="const")
        S.dma(cq, oh2_sb, oh2_d[:, :], writes=["oh2", ("gacc", 0)], sem="const")
        S.dma(cq, mrow_sb, mrow_d[:, :], writes=["mrow", ("gacc", 2)], sem="const")
        S.dma(cq, aident[:], aident_d[:, :], writes=["aident"], sem="const")
        S.seal("const")
        V(lambda e: e.memset(ones_f[:], 1.0), w=["ones"])
        V(lambda e: e.memset(hstate[:], 0.0), w=["hstate"])
        V(lambda e: e.memset(hal[:], 0.0), w=["hal"])
        for gi, (wd, key) in enumerate(((wga_d, "wbd0"), (wgx_d, "wbd1"))):
            G(lambda e, gi=gi: e.memset(wbd[gi][:], 0.0), w=[key])
            wv = wd.ap().rearrange("(c two) i j -> two i c j", two=2)
            for par in range(2):
                S.dma("pool", wbd[gi][par * 64:(par + 1) * 64, :, par * 64:(par + 1) * 64], wv[par],
                      writes=[key], sem="constp")
        S.seal("constp")
        A(lambda e: e.activation(cneg[:], cvec[:, C_LAM:C_LAM + 8], AF.Exp, scale=-1.0), r=["cvec"], w=["cneg"])
        A(lambda e: e.activation(cneg[:], cneg[:], AF.Ln, bias=1.0), r=["cneg"], w=["cneg"])
        V(lambda e: e.tensor_scalar(cneg[:], cneg[:], -8.0, None, ALU.mult), r=["cneg"], w=["cneg"])

        T(lambda e: e.matmul(mm[0:16, 0, 0:384], lhsT=tab_sb[:], rhs=oh2_sb, start=True, stop=True),
          r=["tab", "oh2", ("gacc", 0)], w=["mm0"])
        V(lambda e: e.tensor_tensor(tr_sb, mm[0:16, 0, 0:384], mrow_sb, ALU.add), r=["mm0", "mrow", ("gacc", 2)], w=["tr_sb", ("gacc", 1)])
        S.dma("sp", tr_d[:, :], tr_sb, reads=["tr_sb", ("gacc", 1)], writes=["tr_d"], sem="const")
        wwin = big[:, 8448:8448 + 8192].bitcast(F32).rearrange("p (h s) -> p h s", s=256)
        ggk = [("gg", c) for c in range(8)]
        S.dma("sp", wwin, bass.AP(tr_d, 0, [[1, 128], [384, 16], [1, 256]]), reads=["tr_d"], writes=ggk, sem="const")
        for h in range(16):
            b = h % 2
            T(lambda e, h=h, b=b: e.matmul(sc[:, b, 0:256], lhsT=aident[:], rhs=wwin[:, h, :], start=True, stop=True),
              r=ggk + ["aident"], w=[f"sc{b}"])
            V(lambda e, h=h, b=b: e.tensor_copy(bias[:, h, :], sc[:, b, 0:256]), r=[f"sc{b}"], w=["bias"])

        def wslab(w, r0, c0):
            return (w, r0, c0)

        def slab_ap(src):
            w, r0, c0 = src
            return w.ap()[r0:r0 + 2048, c0:c0 + 512].rearrange("(k p) c -> p k c", p=128)

        plan = []
        if dbg.get("prepass", True):
            for blk_ in range(NBLK):
                if blk_ == 0:
                    plan += [(w_in, 0, s * 512) for s in (0, 1)]
                plan += [(w_in, 0, s * 512) for s in (0, 1)]
        for blk_ in range(NBLK):
            if blk_ == 0:
                plan += [(w_in, 0, s * 512) for s in (0, 1, 6)]
            plan += [(w_in, 0, s * 512) for s in range(7)]
            plan += [(w_out, 0, dg * 512) for dg in range(4)]
            plan += [(w_up, 0, isval * 6144 + sp_ * 512) for sp_ in range(12) for isval in range(2)]
            plan += [(w_down, rg * 2048, dg * 512) for dg in range(4) for rg in range(3)]
        if dbg.get("sample", True):
            plan += [(w_in, 0, s_ * 512) for s_ in range(7)]
            plan += [(w_out, 0, dg * 512) for dg in range(4)]
            plan += [(w_up, 0, isval * 6144 + sp_ * 512) for sp_ in range(12) for isval in range(2)]
            plan += [(w_down, rg * 2048, dg * 512) for dg in range(4) for rg in range(3)]
        slab_state = {"next_use": 0, "next_load": 0}

        def _issue_load():
            k = slab_state["next_load"]
            if k >= len(plan):
                return
            i = k % 2
            w_, r0_, c0_ = plan[k]
            if False:
                for a_ in range(2):
                    for g_ in range(4):
                        cs = c0_ + a_ * 256 + g_ * 64
                        src = w_.ap()[r0_:r0_ + 2048, cs:cs + 64].rearrange("(k p) d -> p k d", p=128)
                        dst = slabs[i][:, :, g_ * 128 + a_ * 64:g_ * 128 + a_ * 64 + 64]
                        S.dma("pool", dst, src, writes=[f"slab{i}"], sem=f"slab{i}")
            else:
                S.dma("pool", slabs[i][:], slab_ap(plan[k]), writes=[f"slab{i}"], sem=f"slab{i}")
            slab_state["next_load"] = k + 1

        def load_slab(src):
            k = slab_state["next_use"]
            assert plan[k][1:] == src[1:] and plan[k][0] is src[0], (k, plan[k][1:], src[1:])
            while slab_state["next_load"] <= k:
                _issue_load()
            slab_state["next_use"] = k + 1
            return k % 2

        def prefetch_next():
            if slab_state["next_load"] <= slab_state["next_use"]:
                _issue_load()

        mm_rot = [0]

        def next_mm(n=3):
            i = mm_rot[0] % n
            mm_rot[0] += 1
            return i

        tp_rot = [0]

        def norm_to_fm(src_tile_ap, src_keys, gcol, tile, dst=fm, dst_keyf=None, ntok=128):
            n = ntok
            A(lambda e: e.activation(xs[0:n, :], src_tile_ap, AF.Square, accum_out=stat[0:n, 0:1]),
              r=src_keys, w=["xs", "stat0"])
            A(lambda e: e.activation(stat[0:n, 1:2], stat[0:n, 0:1], AF.Sqrt, scale=1.0 / D, bias=EPS),
              r=["stat0"], w=["stat1"])
            V(lambda e: e.reciprocal(stat[0:n, 2:3], stat[0:n, 1:2]), r=["stat1"], w=["stat2"])
            V(lambda e: e.tensor_scalar(xs[0:n, :], src_tile_ap, stat[0:n, 2:3], None, ALU.mult),
              r=list(src_keys) + ["stat2"], w=["xs"])
            for grp in range(4):
                b = tp_rot[0] % 2
                tp_rot[0] += 1
                for c4 in range(4):
                    c = grp * 4 + c4
                    T(lambda e, b=b, c4=c4, c=c: e.transpose(tp[:, b, c4 * 128:c4 * 128 + n], xs[0:n, c * 128:(c + 1) * 128], ident[0:n, 0:n]),
                      r=["xs", "ident"], w=[f"tp{b}"], signal=(c4 == 3))
                dk = [dst_keyf(c, tile) for c in range(grp * 4, grp * 4 + 4)]
                V(lambda e, b=b, grp=grp: e.tensor_tensor(
                    dst[:, grp * 4:(grp + 1) * 4, tile * 128:tile * 128 + n],
                    tp[:, b, :].rearrange("p (c t) -> p c t", t=128)[:, :, 0:n],
                    cvec[:, gcol + grp * 4:gcol + grp * 4 + 4].unsqueeze(2).to_broadcast([128, 4, n]), ALU.mult),
                  r=[f"tp{b}", "cvec"], w=dk)

        def dump(tag, ap, keys):
            if not dbg.get("dump"):
                return
            shp = list(ap.shape)
            d_ = nc.dram_tensor("dbg_" + tag, shp, ap.dtype, kind="ExternalOutput")
            S.dma("sp", d_.ap(), ap, reads=keys, sem="dbg_" + tag)

        def fmk(c, t):
            return ("fm", c, t)

        def fm_all(c):
            return [("fm", c, t) for t in range(4)]

        def prompt_block(blk, pre=False):
            last = blk == NBLK - 1
            if blk > 0:
                if not pre:
                    V(lambda e: e.tensor_copy(kT[:, :, 0:128], kT[:, :, BT:BT + 128]), r=["kT"], w=["kTh"])
                    V(lambda e: e.tensor_copy(vtok[:, 0, :], vtok[:, 4, :]), r=[("vtok", 4)], w=[("vtok", 0)])
                V(lambda e: e.tensor_copy(xr[:, :, 0:3], xrt[:]), r=["xrt"], w=["xrh"] + [("act", f) for f in range(17)])
            if blk == 0:
                S.dma("sp", resid[:, 3, :], xp[0:128, :], writes=[("resid", 3)], sem="x3")
                norm_to_fm(resid[:, 3, :], [("resid", 3)], C_GMIX, 3, dst_keyf=fmk)
            for t in range(4):
                if blk == 0 and t == 3:
                    continue
                r0 = 128 + blk * BT + t * 128
                S.dma("sp", resid[:, t, :], xp[r0:r0 + 128, :], writes=[("resid", t)], sem=f"x{t}")
            if blk == 0:
                halo_inproj((0, 1) if pre else (0, 1, 6))
                r0 = 128 + 3 * 128
                S.dma("sp", resid[:, 3, :], xp[r0:r0 + 128, :], writes=[("resid", 3)], sem="x3")
            for t in range(4):
                norm_to_fm(resid[:, t, :], [("resid", t)], C_GMIX, t, dst_keyf=fmk)
            if pre:
                inproj(blk, pre=True)
                return
            if blk == 0:
                dump("xnT", fm[:], [fmk(c, t) for c in range(16) for t in range(4)])
                dump("stat", stat[:], ["stat0", "stat1", "stat2"])
                dump("xs", xs[:], ["xs"])
                dump("x3", resid[:, 3, :], [("resid", 3)])
            if upto == 'norm':
                return
            inproj(blk)
            if blk == 0:
                dump("qT", qT, [("qT", c) for c in range(8)])
                dump("kT", kT[:], ["kT", "kTh"])
                dump("vtok", vtok[:], [("vtok", i) for i in range(5)])
                dump("yrnn", gg, [("gg", c) for c in range(8)])
                dump("mrnn", fm[:, 0:8, :], [fmk(c, t) for c in range(8) for t in range(4)])
                dump("bias", bias[:], ["bias"])
            if upto == 'inproj':
                return
            for t in range(4):
                attention(blk, t)
            if blk == 0:
                dump("merged", fm[:], [fmk(c, t) for c in range(16) for t in range(4)])
            if upto == 'attn':
                return
            outproj()
            if blk == 0:
                S.dma("sp", hfix_d[:, :], resid[0:2, 0, :], reads=[("resid", 0)], writes=["hfix_d"], sem="hfix")
                dump("h", resid[:], [("resid", t) for t in range(4)])
            if upto == 'outproj':
                return
            for t in range(4):
                norm_to_fm(resid[:, t, :], [("resid", t)], C_GFFN, t, dst_keyf=fmk)
            upproj(blk)
            if blk == 0:
                dump("act", act, [("act", f) for f in range(48)])
            if upto == 'up':
                return
            if blk == NBLK - 1 and dbg.get("sample", True):
                S.dma("sp", ex2_in[:, :], hal[:].rearrange("p f t -> p (f t)"), reads=[("hal", f) for f in range(96)], writes=["ex2_in"], sem="ex2_in")
                S.custom("pool", lambda e: e.collective_compute("AllGather", ALU.bypass, replica_groups=[list(range(NCORES))],
                                                                ins=[ex2_in.ap().opt()], outs=[ex2_out.ap().opt()]),
                         reads=["ex2_in"], writes=["ex2_out"], sem="cc2", inc=1)
            S.dma("sp", xs[:], gfin_d[0:1, :].partition_broadcast(128), writes=["xs"], sem="gfin")
            downproj(blk)

        def halo_inproj(slist):
            hc = slice(3 * 128, 4 * 128)
            for s in slist:
                i = load_slab(wslab(w_in, 0, s * 512))
                sk = f"slab{i}"
                if s < 2:
                    b = next_mm()
                    for j in range(4):
                        for k in range(16):
                            T(lambda e, b=b, j=j, k=k, i=i: e.matmul(mm[:, b, j * 4:j * 4 + 3], lhsT=slabs[i][:, k, j * 128:(j + 1) * 128],
                                                                      rhs=fm[:, k, 3 * 128 + 125:3 * 128 + 128], start=(k == 0), stop=(k == 15)),
                              r=[sk, fmk(k, 3)], w=[f"mm{b}"], signal=(k == 15))
                    V(lambda e, b=b, s=s: e.tensor_copy(xr[:, s * 4:(s + 1) * 4, 0:3],
                                                        mm[:, b, 0:16].rearrange("p (j t) -> p j t", t=4)[:, :, 0:3]),
                      r=[f"mm{b}"], w=["xrh"])
                else:
                    for kp in range(2):
                        b = next_mm()
                        for k in range(16):
                            T(lambda e, b=b, kp=kp, k=k, i=i: e.matmul(mm[:, b, 0:128], lhsT=slabs[i][:, k, kp * 128:(kp + 1) * 128],
                                                                        rhs=fm[:, k, hc], start=(k == 0), stop=(k == 15)),
                              r=[sk, fmk(k, 3)], w=[f"mm{b}"], signal=(k == 15))
                        A(lambda e, b=b, kp=kp: e.copy(kT[:, kp, 0:128], mm[:, b, 0:128]), r=[f"mm{b}"], w=["kTh"])
                    b = next_mm()
                    for k in range(16):
                        T(lambda e, b=b, k=k, i=i: e.matmul(mm[:, b, :], lhsT=fm[:, k, hc], rhs=slabs[i][:, k, :],
                                                             start=(k == 0), stop=(k == 15)),
                          r=[sk, fmk(k, 3)], w=[f"mm{b}"], signal=(k == 15))
                    V(lambda e, b=b: e.tensor_copy(vtok[:, 0, :], mm[:, b, 256:512]), r=[f"mm{b}"], w=[("vtok", 0)])

        def inproj(blk, pre=False):
            last = (blk == NBLK - 1) and not dbg.get("nolast")
            for s in (range(2) if pre else range(7)):
                i = load_slab(wslab(w_in, 0, s * 512))
                sk = f"slab{i}"
                if s < 6:
                    for j in range(4):
                        b = next_mm()
                        if s in (4, 5):
                            for k in range(16):
                                for a_ in range(2):
                                    hc = (a_ * 4 + j) * 64
                                    T(lambda e, b=b, k=k, a_=a_, hc=hc, i=i: e.matmul(mm[a_ * 64:(a_ + 1) * 64, b, :], lhsT=slabs[i][:, k, hc:hc + 64],
                                                                                       rhs=fm[:, k, :], start=(k == 0), stop=(k == 15)),
                                      r=[sk] + fm_all(k), w=[f"mm{b}"], signal=(k == 15 and a_ == 1))
                        else:
                            for k in range(16):
                                T(lambda e, b=b, k=k, j=j, i=i: e.matmul(mm[:, b, :], lhsT=slabs[i][:, k, j * 128:(j + 1) * 128], rhs=fm[:, k, :],
                                                                          start=(k == 0), stop=(k == 15)),
                                  r=[sk] + fm_all(k), w=[f"mm{b}"], signal=(k == 15))
                        c = (s % 2) * 4 + j
                        if s < 2:
                            A(lambda e, b=b, c=c: e.copy(xr[:, c, 3:3 + BT], mm[:, b, :]), r=[f"mm{b}"], w=[("xr", c)])
                        elif s < 4:
                            A(lambda e, b=b, c=c: e.activation(gg[:, c, :], mm[:, b, :], AF.Gelu), r=[f"mm{b}"], w=[("gg", c)])
                        else:
                            A(lambda e, b=b, c=c: e.activation(qT[:, c, :], mm[:, b, :], AF.Copy, scale=0.125), r=[f"mm{b}"], w=[("qT", c)])
                    if s == 1:
                        V(lambda e: e.tensor_copy(xrt[:], xr[:, :, BT:BT + 3]), r=[("xr", c) for c in range(8)], w=["xrt"])
                    if pre and s == 1:
                        for c in range(8):
                            rnn_chunk(c, main=False)
                    if s in (2, 3, 4, 5):
                        for c in (2 * (s - 2), 2 * (s - 2) + 1):
                            rnn_chunk(c, main=True)
                else:
                    for kp in range(2):
                        b = next_mm()
                        for k in range(16):
                            T(lambda e, b=b, kp=kp, k=k, i=i: e.matmul(mm[:, b, :], lhsT=slabs[i][:, k, kp * 128:(kp + 1) * 128],
                                                                        rhs=fm[:, k, :], start=(k == 0), stop=(k == 15)),
                              r=[sk] + fm_all(k), w=[f"mm{b}"], signal=(k == 15))
                        A(lambda e, b=b, kp=kp: e.copy(kT[:, kp, 128:128 + BT], mm[:, b, :]), r=[f"mm{b}"], w=["kT"])
                    for t in range(4):
                        b = next_mm()
                        for k in range(16):
                            T(lambda e, b=b, k=k, t=t, i=i: e.matmul(mm[:, b, :], lhsT=fm[:, k, t * 128:(t + 1) * 128], rhs=slabs[i][:, k, :],
                                                                      start=(k == 0), stop=(k == 15)),
                              r=[sk, fmk(k, t)], w=[f"mm{b}"], signal=(k == 15))
                        V(lambda e, b=b, t=t: e.tensor_copy(vtok[:, 1 + t, :], mm[:, b, 256:512]), r=[f"mm{b}"], w=[("vtok", 1 + t)])
                        if last and t == 3:
                            A(lambda e, b=b: e.copy(kvf[:], mm[:, b, :]), r=[f"mm{b}"], w=["kvf"])
                            S.dma("sp", o_pk[:, :], kvf[:, 0:256], reads=["kvf"], sem="o_pk")
                            S.dma("sp", o_pv[:, :], kvf[:, 256:512], reads=["kvf"], sem="o_pv")
            if not pre:
                rnn_finish()

        def rnn_chunk(c, main):
            xc, ra, ib, tb, hb, ysq = (rt[n] for n in ("xc", "ra", "i", "t", "hb", "ysq"))
            cw = lambda k: cvec[:, C_CW + k * 8 + c:C_CW + k * 8 + c + 1]
            xk = ("xr", c)
            A(lambda e: e.activation(xc[:], xr[:, c, 3:3 + BT], AF.Identity, scale=cw(3), bias=cvec[:, C_CB + c:C_CB + c + 1]),
              r=[xk, "cvec"], w=["xc"])
            V(lambda e: e.scalar_tensor_tensor(xc[:], xr[:, c, 2:2 + BT], cw(2), xc[:], ALU.mult, ALU.add), r=[xk, "xrh", "xc"], w=["xc"])
            V(lambda e: e.scalar_tensor_tensor(xc[:], xr[:, c, 1:1 + BT], cw(1), xc[:], ALU.mult, ALU.add), r=[xk, "xrh", "xc"], w=["xc"])
            V(lambda e: e.scalar_tensor_tensor(xc[:], xr[:, c, 0:BT], cw(0), xc[:], ALU.mult, ALU.add), r=[xk, "xrh", "xc"], w=["xc"])
            G(lambda e: e.tensor_copy(xcb[:], xc[:]), r=["xc"], w=["xcb"])
            T(lambda e: e.matmul(sc[:, 0, :], lhsT=wbd[0][:, c, :], rhs=xcb[:], start=True, stop=True), r=["xcb", "wbd0"], w=["sc0"])
            T(lambda e: e.matmul(sc[:, 1, :], lhsT=wbd[1][:, c, :], rhs=xcb[:], start=True, stop=True), r=["xcb", "wbd1"], w=["sc1"])
            if main:
                A(lambda e: e.activation(ra[:], sc[:, 0, :], AF.Sigmoid, bias=cvec[:, C_BA + c:C_BA + c + 1]), r=["sc0", "cvec"], w=["ra"])
            else:
                A(lambda e: e.activation(ra[:], sc[:, 0, :], AF.Sigmoid, bias=cvec[:, C_BA + c:C_BA + c + 1], accum_out=rsum[:, c:c + 1]),
                  r=["sc0", "cvec"], w=["ra", "rsum"])
                V(lambda e: e.tensor_tensor(rtot[:, c:c + 1], rtot[:, c:c + 1], rsum[:, c:c + 1], ALU.add), r=["rsum", "rtot"], w=["rtot"])
            A(lambda e: e.activation(ib[:], sc[:, 1, :], AF.Sigmoid, bias=cvec[:, C_BX + c:C_BX + c + 1]), r=["sc1", "cvec"], w=["ib"])
            A(lambda e: e.activation(ra[:], ra[:], AF.Exp, scale=cneg[:, c:c + 1]), r=["ra", "cneg"], w=["ra"])
            V(lambda e: e.tensor_tensor(tb[:], ra[:], ra[:], ALU.mult), r=["ra"], w=["tb"])
            A(lambda e: e.activation(tb[:], tb[:], AF.Sqrt, scale=-1.0, bias=1.0), r=["tb"], w=["tb"])
            G(lambda e: e.tensor_tensor(ib[:], ib[:], xc[:], ALU.mult), r=["ib", "xc"], w=["ib"])
            V(lambda e: e.tensor_tensor(ib[:], ib[:], tb[:], ALU.mult), r=["ib", "tb"], w=["ib"])
            V(lambda e: e.tensor_tensor_scan(hb[:], ra[:], ib[:], hstate[:, c:c + 1], ALU.mult, ALU.add), r=["ra", "ib", "hstate"], w=["hb"])
            V(lambda e: e.tensor_copy(hstate[:, c:c + 1], hb[:, BT - 1:BT]), r=["hb"], w=["hstate"])
            if main:
                G(lambda e: e.tensor_tensor(gg[:, c, :], gg[:, c, :], hb[:], ALU.mult), r=[("gg", c), "hb"], w=[("gg", c)])
                G(lambda e: e.tensor_tensor(ysq[:], gg[:, c, :], gg[:, c, :], ALU.mult), r=[("gg", c)], w=["ysq"])
                T(lambda e: e.matmul(mm[:, 3, :], lhsT=ones_f[:], rhs=ysq[:], start=(c == 0), stop=(c == 7)),
                  r=["ysq", "ones"], w=["mm3"], signal=True)

        def rnn_finish():
            A(lambda e: e.activation(rr[:], mm[:, 3, :], AF.Sqrt, scale=1.0 / DR, bias=EPS), r=["mm3"], w=["rr"])
            V(lambda e: e.reciprocal(rr[:], rr[:]), r=["rr"], w=["rr"])
            for c in range(8):
                V(lambda e, c=c: e.scalar_tensor_tensor(fm[:, c, :], gg[:, c, :], cvec[:, C_GRNN + c:C_GRNN + c + 1], rr[:], ALU.mult, ALU.mult),
                  r=[("gg", c), "rr", "cvec"], w=fm_all(c))

        def attention(blk, t):
            first = (blk == 0 and t == 0)
            o_ps = mm[:, 0:2, :].rearrange("p a b -> p (a b)")
            for kp in range(2):
                for g in range(4):
                    for half in range(2):
                        h = 8 * kp + 4 * half + g
                        idx = 4 * half + g
                        ps = slice(half * 64, (half + 1) * 64)
                        T(lambda e, half=half, g=g, ps=ps, kp=kp: e.matmul(
                            sc[:, half, 0:256], lhsT=qT[ps, kp * 4 + g, t * 128:(t + 1) * 128], rhs=kT[ps, kp, t * 128:t * 128 + 256],
                            start=True, stop=True),
                          r=[("qT", kp * 4 + g), "kT", "kTh"], w=[f"sc{half}"])
                        V(lambda e, half=half, idx=idx, h=h: e.tensor_tensor(s_half[:, idx, :], sc[:, half, 0:256], bias[:, h, :], ALU.add),
                          r=[f"sc{half}", "bias"], w=["xs"])
                if first:
                    V(lambda e: e.tensor_scalar(s_half[:, :, 0:128], s_half[:, :, 0:128], cflag[:, 9:10], None, ALU.add), r=["xs", "cflag"], w=["xs"])
                hs = slice(8 * kp, 8 * kp + 8)
                V(lambda e, hs=hs: e.tensor_reduce(m8[:, hs], s_half[:], AX.X, ALU.max), r=["xs"], w=["m8"])
                V(lambda e, hs=hs: e.tensor_tensor(m8[:, hs], m8[:, hs], sinks_b[:, hs], ALU.max), r=["m8", "sinks_b"], w=["m8"])
                V(lambda e, hs=hs: e.tensor_scalar(negm[:, hs], m8[:, hs], -1.0, None, ALU.mult), r=["m8"], w=["negm"])
                V(lambda e, hs=hs: e.tensor_tensor(es[:, hs], sinks_b[:, hs], m8[:, hs], ALU.subtract), r=["m8", "sinks_b"], w=["es"])
                A(lambda e, hs=hs: e.activation(es[:, hs], es[:, hs], AF.Exp), r=["es"], w=["es"])
                for idx in range(8):
                    h = 8 * kp + idx
                    kvh = h // 4
                    pi = idx % 2
                    A(lambda e, idx=idx, h=h, pi=pi: e.activation(pbuf[:, pi, :], s_half[:, idx, :], AF.Exp, bias=negm[:, h:h + 1],
                                                                   accum_out=rs[:, h:h + 1]),
                      r=["xs", "negm"], w=[("pbuf", pi), ("rs", h)])
                    tpx = mm[:, 2 + pi, :].bitcast(BF16)
                    for kb in range(2):
                        T(lambda e, pi=pi, kb=kb, tpx=tpx: e.transpose(tpx[:, kb * 128:(kb + 1) * 128], pbuf[:, pi, kb * 128:(kb + 1) * 128],
                                                                       ident_b[:]),
                          r=[("pbuf", pi), "ident_b"], w=[f"mm{2 + pi}"], signal=(kb == 1))
                    V(lambda e, pi=pi, tpx=tpx: e.tensor_copy(ptb[:, pi, :, :], tpx[:, 0:256].rearrange("p (a b) -> p a b", b=128)),
                      r=[f"mm{2 + pi}"], w=[("ptb", pi)])
                    for kb in range(2):
                        T(lambda e, pi=pi, kb=kb, h=h, kvh=kvh: e.matmul(o_ps[:, h * 64:(h + 1) * 64], lhsT=ptb[:, pi, kb, :],
                                                                          rhs=vtok[:, t + kb, kvh * 64:(kvh + 1) * 64], start=(kb == 0), stop=(kb == 1)),
                          r=[("ptb", pi), ("vtok", t + kb)], w=["mm0", "mm1"], signal=(kb == 1))
                V(lambda e, hs=hs: e.tensor_tensor(rden[:, hs], rs[:, hs], es[:, hs], ALU.add), r=[("rs", h) for h in range(8 * kp, 8 * kp + 8)] + ["es"], w=["rden"])
                V(lambda e, hs=hs: e.reciprocal(rden[:, hs], rden[:, hs]), r=["rden"], w=["rden"])
            yat = xs[:, 0:1024]
            V(lambda e: e.tensor_tensor(yat.rearrange("p (h d) -> p h d", d=64), o_ps.rearrange("p (h d) -> p h d", d=64),
                                        rden[:].unsqueeze(2).to_broadcast([128, 16, 64]), ALU.mult),
              r=["mm0", "mm1", "rden"], w=["xs"])
            A(lambda e: e.activation(xs[:, 1024:2048], yat, AF.Square, accum_out=stat[:, 4:5]), r=["xs"], w=["xs2", "stat4"])
            A(lambda e: e.activation(stat[:, 5:6], stat[:, 4:5], AF.Sqrt, scale=1.0 / DR, bias=EPS), r=["stat4"], w=["stat5"])
            V(lambda e: e.reciprocal(stat[:, 6:7], stat[:, 5:6]), r=["stat5"], w=["stat6"])
            V(lambda e: e.tensor_scalar(yat, yat, stat[:, 6:7], None, ALU.mult), r=["xs", "stat6"], w=["xs"])
            for grp in range(2):
                for c4 in range(4):
                    c = grp * 4 + c4
                    T(lambda e, c4=c4, c=c: e.transpose(tp[:, 0, c4 * 128:(c4 + 1) * 128], xs[:, c * 128:(c + 1) * 128], ident[:]),
                      r=["xs", "ident"], w=["tp0"], signal=(c4 == 3))
                V(lambda e, grp=grp: e.tensor_tensor(
                    fm[:, 8 + grp * 4:8 + (grp + 1) * 4, t * 128:(t + 1) * 128],
                    tp[:, 0, :].rearrange("p (c t) -> p c t", t=128),
                    cvec[:, C_GATT + grp * 4:C_GATT + grp * 4 + 4].unsqueeze(2).to_broadcast([128, 4, 128]), ALU.mult),
                  r=["tp0", "cvec"], w=[fmk(8 + grp * 4 + c4, t) for c4 in range(4)])

        def outproj():
            for dg in range(4):
                i = load_slab(wslab(w_out, 0, dg * 512))
                sk = f"slab{i}"
                for e_ in range(16):
                    for t in range(4):
                        T(lambda e, e_=e_, t=t, i=i: e.matmul(mm[:, t, :], lhsT=fm[:, e_, t * 128:(t + 1) * 128], rhs=slabs[i][:, e_, :],
                                                               start=(e_ == 0), stop=(e_ == 15)),
                          r=[sk, fmk(e_, t)], w=[f"mm{t}"], signal=(e_ == 15))
                for t in range(4):
                    V(lambda e, t=t, dg=dg: e.tensor_tensor(resid[:, t, dg * 512:(dg + 1) * 512], mm[:, t, :], resid[:, t, dg * 512:(dg + 1) * 512], ALU.add),
                      r=[f"mm{t}", ("resid", t)], w=[("resid", t)])

        def upproj(blk):
            last = blk == NBLK - 1
            for sp_ in range(12):
                for isval in range(2):
                    i = load_slab(wslab(w_up, 0, isval * 6144 + sp_ * 512))
                    sk = f"slab{i}"
                    for j in range(4):
                        f = isval * 48 + sp_ * 4 + j
                        b = next_mm(4)
                        for k in range(16):
                            T(lambda e, b=b, k=k, j=j, i=i: e.matmul(mm[:, b, :], lhsT=slabs[i][:, k, j * 128:(j + 1) * 128], rhs=fm[:, k, :],
                                                                      start=(k == 0), stop=(k == 15)),
                              r=[sk] + fm_all(k), w=[f"mm{b}"], signal=(k == 15))
                        acc = vacc[:, j % 2, :] if isval else gacc[:, j, :]
                        ak = ("vacc", j % 2) if isval else ("gacc", j)
                        fw = lambda k, f=f: cvec[:, C_FW + k * 96 + f:C_FW + k * 96 + f + 1]
                        mk = f"mm{b}"
                        A(lambda e, b=b, acc=acc, fw=fw, f=f: e.activation(acc, mm[:, b, :], AF.Identity, scale=fw(2), bias=cvec[:, C_FB + f:C_FB + f + 1]),
                          r=[mk, "cvec"], w=[ak])
                        V(lambda e, b=b, acc=acc, fw=fw: e.scalar_tensor_tensor(acc[:, 1:BT], mm[:, b, 0:BT - 1], fw(1), acc[:, 1:BT], ALU.mult, ALU.add),
                          r=[mk, ak], w=[ak])
                        V(lambda e, b=b, acc=acc, fw=fw: e.scalar_tensor_tensor(acc[:, 2:BT], mm[:, b, 0:BT - 2], fw(0), acc[:, 2:BT], ALU.mult, ALU.add),
                          r=[mk, ak], w=[ak])
                        V(lambda e, acc=acc, fw=fw, f=f: e.scalar_tensor_tensor(acc[:, 0:2], hal[:, f, :], fw(0), acc[:, 0:2], ALU.mult, ALU.add),
                          r=[("hal", f), ak], w=[ak])
                        V(lambda e, acc=acc, fw=fw, f=f: e.scalar_tensor_tensor(acc[:, 0:1], hal[:, f, 1:2], fw(1), acc[:, 0:1], ALU.mult, ALU.add),
                          r=[("hal", f), ak], w=[ak])
                        V(lambda e, b=b, f=f: e.tensor_copy(hal[:, f, :], mm[:, b, BT - 2:BT]), r=[mk], w=[("hal", f)])
                        if blk == 0:
                            V(lambda e, b=b, f=f: e.tensor_copy(xpfix[:, f, 2:4], mm[:, b, 0:2]), r=[mk], w=[("xpfix", f)])
                        if not isval:
                            A(lambda e, acc=acc: e.activation(acc, acc, AF.Gelu), r=[ak], w=[ak])
                        else:
                            fa = sp_ * 4 + j
                            G(lambda e, acc=acc, j=j, fa=fa: e.tensor_tensor(act[:, fa, :], gacc[:, j, :], acc, ALU.mult),
                              r=[ak, ("gacc", j)], w=[("act", fa)])

        def downproj(blk):
            for dg in range(4):
                for rg in range(3):
                    i = load_slab(wslab(w_down, rg * 2048, dg * 512))
                    sk = f"slab{i}"
                    for fl in range(16):
                        f = rg * 16 + fl
                        for t in range(4):
                            T(lambda e, f=f, fl=fl, t=t, i=i: e.matmul(mm[:, t, :], lhsT=act[:, f, t * 128:(t + 1) * 128], rhs=slabs[i][:, fl, :],
                                                                        start=(f == 0), stop=(f == 47)),
                              r=[sk, ("act", f)], w=[f"mm{t}"], signal=(f == 47 or (fl == 15 and t == 3)))
                for t in range(4):
                    V(lambda e, t=t, dg=dg: e.tensor_tensor(resid[:, t, dg * 512:(dg + 1) * 512], mm[:, t, :], resid[:, t, dg * 512:(dg + 1) * 512], ALU.add),
                      r=[f"mm{t}", ("resid", t)], w=[("resid", t)])
            for t in range(4):
                A(lambda e, t=t: e.activation(gacc[:].rearrange("p a b -> p (a b)"), resid[:, t, :], AF.Square, accum_out=stat[:, 8 + t:9 + t]),
                  r=[("resid", t)], w=[("gacc", j) for j in range(4)] + [("st8", t)])
                A(lambda e, t=t: e.activation(stat[:, 12 + t:13 + t], stat[:, 8 + t:9 + t], AF.Sqrt, scale=1.0 / D, bias=EPS), r=[("st8", t)], w=[("st12", t)])
                V(lambda e, t=t: e.reciprocal(stat[:, 16 + t:17 + t], stat[:, 12 + t:13 + t]), r=[("st12", t)], w=[("st16", t)])
                V(lambda e, t=t: e.scalar_tensor_tensor(resid[:, t, :], resid[:, t, :], stat[:, 16 + t:17 + t], xs[:], ALU.mult, ALU.mult),
                  r=[("resid", t), ("st16", t), "xs"], w=[("resid", t)])
                r0 = blk * BT + t * 128
                S.dma("sp", yp[r0:r0 + 128, :], resid[:, t, :], reads=[("resid", t)], sem=f"yo{t}")

        def sample_block():
            NS = 64
            ALLACT = [("act", f) for f in range(48)]
            def bview(off, nbytes, dt=BF16):
                v = big[:, off // 2:(off + nbytes) // 2]
                return v if dt == BF16 else v.bitcast(F32)
            act_s = bview(0, 6336).rearrange("p (f t) -> p f t", t=66)
            xr_s = bview(6400, 3584, F32).rearrange("p (c t) -> p c t", t=112)
            gg_s = bview(10240, 2048, F32).rearrange("p (c t) -> p c t", t=64)
            qbd = bview(12288, 2048).rearrange("p (a s r) -> p a s r", a=2, s=16)
            ktc = bview(14336, 8448).rearrange("p (a s k) -> p a s k", a=2, s=16)
            vcb = bview(22784, 8192).rearrange("p (s d) -> p s d", d=256)
            vnew = bview(30976, 8192).rearrange("p (s d) -> p s d", d=256)
            s_sb = bview(39168, 1056, F32).rearrange("p (a k) -> p a k", a=2)
            p_sb = bview(40224, 528).rearrange("p (a k) -> p a k", a=2)
            ptc = bview(40752, 512).rearrange("p (a q) -> p a q", a=2)[:, :, 0:64]
            ptn = bview(41264, 512).rearrange("p (a q) -> p a q", a=2)[:, :, 0:64]
            o_g = [bview(41776 + i * 2048, 2048, F32).rearrange("p (a d) -> p a d", a=2) for i in range(2)]
            xc_a, ra_a, i_a, t_a, h_a, ysq_a = (rt[n][:].rearrange("p (c t) -> p c t", t=64) for n in ("xc", "ra", "i", "t", "hb", "ysq"))
            xcb_a = xcb[:].rearrange("p (c t) -> p c t", t=64)
            S.dma("sp", resid[0:NS, 0, :], xs_in[:, :], writes=[("resid", 0)], sem="x0")
            S.dma("sp", resid[64:66, 0, :], hfix_d[:, :], reads=["hfix_d"], writes=[("resid", 0)], sem="x0")
            S.dma("sp", xr_s[:, :, 0:48], src_conv_d.ap(), writes=["xr_s"] + ALLACT, sem="sl0")
            S.dma("sp", h_a[:, :, 0:16], srh_d.ap(), writes=["hb"], sem="sl0")
            S.dma("pool", ktc[:, :, :, 0:128], kcT_d.ap(), writes=["ktc"] + ALLACT, sem="sl1")
            S.dma("pool", vcb, vc_d.ap(), writes=["vcb"] + ALLACT, sem="sl2")
            S.dma("sp", sinkr[:], sinkr_d[:, :], writes=["sinkr"], sem="sl0")
            for t in range(4):
                S.dma("sp", bias_s[t:64:4, :], tr_d[:, 127 - t:127 - t + 132], reads=["tr_d"], writes=["bias_s"], sem="sl0")
            S.seal("sl0")
            S.dma("sp", o_sk[:, 0:124, :], kc_o[:, 4:128, :], sem="osk")
            S.dma("sp", o_sv[:, 0:124, :], vc_o[:, 4:128, :], sem="osv")
            V(lambda e: e.memset(qbd, 0.0), w=["qbd"] + ALLACT)
            norm_to_fm(resid[0:NS, 0, :], [("resid", 0)], C_GMIX, 0, dst_keyf=fmk, ntok=NS)
            fmr = lambda k: [fmk(k, 0)]
            for s_ in range(7):
                i = load_slab(wslab(w_in, 0, s_ * 512))
                sk = f"slab{i}"
                if s_ < 6:
                    for j in range(4):
                        b = next_mm()
                        c = (s_ % 2) * 4 + j
                        if s_ in (4, 5):
                            for k in range(16):
                                for a_ in range(2):
                                    hc = (a_ * 4 + j) * 64
                                    T(lambda e, b=b, k=k, a_=a_, hc=hc, i=i: e.matmul(mm[a_ * 64:(a_ + 1) * 64, b, 0:NS], lhsT=slabs[i][:, k, hc:hc + 64],
                                                                                       rhs=fm[:, k, 0:NS], start=(k == 0), stop=(k == 15)),
                                      r=[sk] + fmr(k), w=[f"mm{b}"], signal=(k == 15 and a_ == 1))
                            kp = s_ - 4
                            for kl in range(2):
                                ps = slice(kl * 64, (kl + 1) * 64)
                                dstq = qbd[ps, kp, :, kl * 16 + j * 4:kl * 16 + j * 4 + 4]
                                A(lambda e, b=b, ps=ps, dstq=dstq: e.activation(dstq, mm[ps, b, 0:NS].rearrange("p (t s) -> p s t", t=4), AF.Copy, scale=0.125),
                                  r=[f"mm{b}"], w=["qbd"])
                        else:
                            for k in range(16):
                                T(lambda e, b=b, k=k, j=j, i=i: e.matmul(mm[:, b, 0:NS], lhsT=slabs[i][:, k, j * 128:(j + 1) * 128], rhs=fm[:, k, 0:NS],
                                                                          start=(k == 0), stop=(k == 15)),
                                  r=[sk] + fmr(k), w=[f"mm{b}"], signal=(k == 15))
                            if s_ < 2:
                                A(lambda e, b=b, c=c: e.copy(xr_s[:, c, 48:112], mm[:, b, 0:NS]), r=[f"mm{b}"], w=["xr_s"])
                            else:
                                A(lambda e, b=b, c=c: e.activation(gg_s[:, c, :], mm[:, b, 0:NS], AF.Gelu), r=[f"mm{b}"], w=["gg_s"])
                    if s_ < 2:
                        b = next_mm()
                        for k in range(16):
                            T(lambda e, b=b, k=k, i=i: e.matmul(mm[0:NS, b, :], lhsT=fm[:, k, 0:NS], rhs=slabs[i][:, k, :], start=(k == 0), stop=(k == 15)),
                              r=[sk] + fmr(k), w=[f"mm{b}"], signal=(k == 15))
                        V(lambda e, b=b: e.tensor_copy(kvf[0:NS, :], mm[0:NS, b, :]), r=[f"mm{b}"], w=["kvf"])
                        S.dma("sp", o_src[:, s_ * 512:(s_ + 1) * 512], kvf[16:64, :], reads=["kvf"], sem="osrc")
                    if s_ == 3:
                        sample_rnn(xr_s, gg_s, xc_a, ra_a, i_a, t_a, h_a, ysq_a, xcb_a)
                else:
                    for kp in range(2):
                        b = next_mm()
                        for k in range(16):
                            T(lambda e, b=b, kp=kp, k=k, i=i: e.matmul(mm[:, b, 0:NS], lhsT=slabs[i][:, k, kp * 128:(kp + 1) * 128], rhs=fm[:, k, 0:NS],
                                                                        start=(k == 0), stop=(k == 15)),
                              r=[sk] + fmr(k), w=[f"mm{b}"], signal=(k == 15))
                        A(lambda e, b=b, kp=kp: e.copy(ktc[:, kp, :, 128:132], mm[:, b, 0:NS].rearrange("p (t s) -> p s t", t=4)), r=[f"mm{b}"], w=["ktc"])
                    b = next_mm()
                    for k in range(16):
                        T(lambda e, b=b, k=k, i=i: e.matmul(mm[0:NS, b, :], lhsT=fm[:, k, 0:NS], rhs=slabs[i][:, k, :], start=(k == 0), stop=(k == 15)),
                          r=[sk] + fmr(k), w=[f"mm{b}"], signal=(k == 15))
                    V(lambda e, b=b: e.tensor_copy(kvf[0:NS, :], mm[0:NS, b, :]), r=[f"mm{b}"], w=["kvf"])
                    V(lambda e, b=b: e.tensor_copy(pbuf[0:NS, 0, :], mm[0:NS, b, 256:512]), r=[f"mm{b}"], w=[("pbuf", 0)])
                    S.dma("sp", vscr[:, :], pbuf[0:NS, 0, :], reads=[("pbuf", 0)], writes=["vscr"], sem="sl3")
                    S.dma("sp", vnew[0:4, :, :], vscr.ap().rearrange("(t s) d -> t s d", t=4), reads=["vscr"], writes=["vnew"] + ALLACT, sem="sl3")
                    for t in range(4):
                        S.dma("sp", o_sk[:, 124 + t, :], kvf[t * 16:(t + 1) * 16, 0:256], reads=["kvf"], sem="osk")
                        S.dma("sp", o_sv[:, 124 + t, :], kvf[t * 16:(t + 1) * 16, 256:512], reads=["kvf"], sem="osv")
            A(lambda e: e.activation(rr[:, 0:NS], mm[:, 3, 0:NS], AF.Sqrt, scale=1.0 / DR, bias=EPS), r=["mm3"], w=["rr"])
            V(lambda e: e.reciprocal(rr[:, 0:NS], rr[:, 0:NS]), r=["rr"], w=["rr"])
            for c in range(8):
                V(lambda e, c=c: e.scalar_tensor_tensor(fm[:, c, 0:NS], gg_s[:, c, :], cvec[:, C_GRNN + c:C_GRNN + c + 1], rr[:, 0:NS], ALU.mult, ALU.mult),
                  r=["gg_s", "rr", "cvec"], w=[fmk(c, 0)])
            for gi in range(8):
                bk = gi % 2
                for si in range(2):
                    sq = gi * 2 + si
                    for kp in range(2):
                        T(lambda e, si=si, sq=sq, kp=kp, bk=bk: e.matmul(sc[kp * 32:(kp + 1) * 32, bk, si * 132:(si + 1) * 132], lhsT=qbd[:, kp, sq, :],
                                                                         rhs=ktc[:, kp, sq, :], start=True, stop=True),
                          r=["qbd", "ktc"], w=[f"sc{bk}"])
                V(lambda e, bk=bk: e.tensor_tensor(s_sb[0:64], sc[0:64, bk, 0:264].rearrange("p (a k) -> p a k", a=2),
                                                   bias_s[:].unsqueeze(1).to_broadcast([64, 2, 132]), ALU.add),
                  r=[f"sc{bk}", "bias_s"], w=["s_sb"])
                V(lambda e: e.tensor_reduce(sst[:, 0:2], s_sb[0:64], AX.X, ALU.max), r=["s_sb"], w=["sst"])
                V(lambda e: e.tensor_scalar(sst[:, 0:2], sst[:, 0:2], sinkr[:, 0:1], None, ALU.max), r=["sst", "sinkr"], w=["sst"])
                V(lambda e: e.tensor_scalar(sst[:, 2:4], sst[:, 0:2], -1.0, None, ALU.mult), r=["sst"], w=["sst"])
                V(lambda e: e.tensor_scalar(sst[:, 4:6], sst[:, 0:2], -1.0, sinkr[:, 0:1], ALU.mult, ALU.add), r=["sst", "sinkr"], w=["sst"])
                A(lambda e: e.activation(sst[:, 4:6], sst[:, 4:6], AF.Exp), r=["sst"], w=["sst"])
                for si in range(2):
                    A(lambda e, si=si: e.activation(p_sb[0:64, si, :], s_sb[0:64, si, :], AF.Exp, bias=sst[:, 2 + si:3 + si], accum_out=sst[:, 6 + si:7 + si]),
                      r=["s_sb", "sst"], w=["p_sb", "sst"])
                V(lambda e: e.tensor_tensor(sst[:, 8:10], sst[:, 6:8], sst[:, 4:6], ALU.add), r=["sst"], w=["sst"])
                V(lambda e: e.reciprocal(sst[:, 8:10], sst[:, 8:10]), r=["sst"], w=["sst"])
                og = o_g[gi % 2]
                ogk = f"o_g{gi % 2}"
                for si in range(2):
                    sq = gi * 2 + si
                    tpx = mm[:, 2 + si, :].bitcast(BF16)
                    T(lambda e, si=si, tpx=tpx: e.transpose(tpx[:, 0:64], p_sb[0:64, si, 0:128], ident_b[0:64, 0:64]),
                      r=["p_sb", "ident_b"], w=[f"mm{2 + si}"], signal=False)
                    T(lambda e, si=si, tpx=tpx: e.transpose(tpx[0:4, 64:128], p_sb[0:64, si, 128:132], ident_b[0:64, 0:64]),
                      r=["p_sb", "ident_b"], w=[f"mm{2 + si}"])
                    V(lambda e, si=si, tpx=tpx: e.tensor_copy(ptc[:, si, :], tpx[:, 0:64]), r=[f"mm{2 + si}"], w=["ptc"])
                    V(lambda e, si=si, tpx=tpx: e.tensor_copy(ptn[0:4, si, :], tpx[0:4, 64:128]), r=[f"mm{2 + si}"], w=["ptn"])
                    T(lambda e, si=si, sq=sq: e.matmul(mm[0:64, si, 0:256], lhsT=ptc[:, si, :], rhs=vcb[:, sq, :], start=True, stop=False),
                      r=["ptc", "vcb"], w=[f"mm{si}"], signal=False)
                    T(lambda e, si=si, sq=sq: e.matmul(mm[0:64, si, 0:256], lhsT=ptn[0:4, si, :], rhs=vnew[0:4, sq, :], start=False, stop=True),
                      r=["ptn", "vnew"], w=[f"mm{si}"])
                    V(lambda e, si=si, og=og: e.tensor_scalar(og[0:64, si, :], mm[0:64, si, 0:256], sst[:, 8 + si:9 + si], None, ALU.mult),
                      r=[f"mm{si}", "sst"], w=[ogk])
                S.dma("sp", oscr[:, gi * 2:gi * 2 + 2, :], og[0:64], reads=[ogk], writes=[("oscr", gi)], sem=f"og{gi % 2}")
            yat = xs[0:NS, 0:1024]
            for t in range(4):
                for kv in range(4):
                    src = bass.AP(oscr, (16 * kv + t) * 4096 + kv * 64, [[256, 16], [4 * 4096, 4], [1, 64]])
                    S.dma("sp", xs[t * 16:(t + 1) * 16, kv * 256:(kv + 1) * 256].rearrange("p (g d) -> p g d", g=4), src,
                          reads=[("oscr", gi_) for gi_ in range(8)], writes=["xs"], sem="yat")
            S.seal("yat")
            dump("s_yat", yat, ["xs"])
            A(lambda e: e.activation(xs[0:NS, 1024:2048], yat, AF.Square, accum_out=stat[0:NS, 4:5]), r=["xs"], w=["xs2", "stat4"])
            A(lambda e: e.activation(stat[0:NS, 5:6], stat[0:NS, 4:5], AF.Sqrt, scale=1.0 / DR, bias=EPS), r=["stat4"], w=["stat5"])
            V(lambda e: e.reciprocal(stat[0:NS, 6:7], stat[0:NS, 5:6]), r=["stat5"], w=["stat6"])
            V(lambda e: e.tensor_scalar(yat, yat, stat[0:NS, 6:7], None, ALU.mult), r=["xs", "stat6"], w=["xs"])
            for grp in range(2):
                for c4 in range(4):
                    c = grp * 4 + c4
                    T(lambda e, c4=c4, c=c: e.transpose(tp[:, 0, c4 * 128:c4 * 128 + NS], xs[0:NS, c * 128:(c + 1) * 128], ident[0:NS, 0:NS]),
                      r=["xs", "ident"], w=["tp0"], signal=(c4 == 3))
                V(lambda e, grp=grp: e.tensor_tensor(
                    fm[:, 8 + grp * 4:8 + (grp + 1) * 4, 0:NS],
                    tp[:, 0, :].rearrange("p (c t) -> p c t", t=128)[:, :, 0:NS],
                    cvec[:, C_GATT + grp * 4:C_GATT + grp * 4 + 4].unsqueeze(2).to_broadcast([128, 4, NS]), ALU.mult),
                  r=["tp0", "cvec"], w=[fmk(8 + grp * 4 + c4, 0) for c4 in range(4)])
            dump("s_merged", fm[:, :, 0:NS], [fmk(c, 0) for c in range(16)])
            dump("s_qbd", qbd, ["qbd"])
            dump("s_ktc", ktc, ["ktc"])
            dump("s_bias", bias_s[:], ["bias_s"])
            dump("s_vnew", vnew[0:4], ["vnew"])
            for dg in range(4):
                i = load_slab(wslab(w_out, 0, dg * 512))
                sk = f"slab{i}"
                b = next_mm(4)
                for e_ in range(16):
                    T(lambda e, e_=e_, i=i, b=b: e.matmul(mm[0:NS, b, :], lhsT=fm[:, e_, 0:NS], rhs=slabs[i][:, e_, :], start=(e_ == 0), stop=(e_ == 15)),
                      r=[sk, fmk(e_, 0)], w=[f"mm{b}"], signal=(e_ == 15))
                V(lambda e, b=b, dg=dg: e.tensor_tensor(resid[0:NS, 0, dg * 512:(dg + 1) * 512], mm[0:NS, b, :], resid[0:NS, 0, dg * 512:(dg + 1) * 512], ALU.add),
                  r=[f"mm{b}", ("resid", 0)], w=[("resid", 0)])
            dump("s_h", resid[0:NS, 0, :], [("resid", 0)])
            norm_to_fm(resid[0:NS, 0, :], [("resid", 0)], C_GFFN, 0, dst_keyf=fmk, ntok=NS)
            V(lambda e: e.memset(xpfix[:, :, 0:2], 0.0), w=[("xpfix", f) for f in range(96)])
            for r_ in range(NCORES):
                S.dma("sp", exh[r_ % 2][:], ex2_out[r_ * 128:(r_ + 1) * 128, :], reads=["ex2_out"], writes=[f"exh{r_ % 2}"], sem=f"exh{r_ % 2}")
                V(lambda e, r_=r_: e.scalar_tensor_tensor(xpfix[:, :, 0:2], exh[r_ % 2][:].rearrange("p (f t) -> p f t", t=2), cflag[:, 10 + r_:11 + r_],
                                                          xpfix[:, :, 0:2], ALU.mult, ALU.add),
                  r=[f"exh{r_ % 2}", "cflag"], w=[("xpfix", f) for f in range(96)])
            for sp_ in range(12):
                for isval in range(2):
                    i = load_slab(wslab(w_up, 0, isval * 6144 + sp_ * 512))
                    sk = f"slab{i}"
                    for j in range(4):
                        f = isval * 48 + sp_ * 4 + j
                        b = next_mm(4)
                        for k in range(16):
                            T(lambda e, b=b, k=k, j=j, i=i: e.matmul(mm[:, b, 0:NS], lhsT=slabs[i][:, k, j * 128:(j + 1) * 128], rhs=fm[:, k, 0:NS],
                                                                      start=(k == 0), stop=(k == 15)),
                              r=[sk, fmk(k, 0)], w=[f"mm{b}"], signal=(k == 15))
                        xp_ = xpf[f % 2]
                        xk = f"xpf{f % 2}"
                        S.dma("sp", xp_[:, 0:32], sfc_d[:, f, :], writes=[xk], sem=f"sfc{f % 2}")
                        A(lambda e, b=b, xp_=xp_: e.copy(xp_[:, 32:96], mm[:, b, 0:NS]), r=[f"mm{b}"], w=[xk])
                        acc = (accs[2][:] if isval else accs[j % 2][:])
                        ak = "accs2" if isval else f"accs{j % 2}"
                        if not isval:
                            acc = gacc[:, j, 0:66]
                            ak = ("gacc", j)
                        fw = lambda k, f=f: cvec[:, C_FW + k * 96 + f:C_FW + k * 96 + f + 1]
                        fb = cvec[:, C_FB + f:C_FB + f + 1]
                        A(lambda e, acc=acc, xp_=xp_, fw=fw, fb=fb: e.activation(acc[:, 0:64], xp_[:, 32:96], AF.Identity, scale=fw(2), bias=fb), r=[xk, "cvec"], w=[ak])
                        V(lambda e, acc=acc, xp_=xp_, fw=fw: e.scalar_tensor_tensor(acc[:, 0:64], xp_[:, 16:80], fw(1), acc[:, 0:64], ALU.mult, ALU.add), r=[xk, ak], w=[ak])
                        V(lambda e, acc=acc, xp_=xp_, fw=fw: e.scalar_tensor_tensor(acc[:, 0:64], xp_[:, 0:64], fw(0), acc[:, 0:64], ALU.mult, ALU.add), r=[xk, ak], w=[ak])
                        A(lambda e, acc=acc, f=f, fw=fw, fb=fb: e.activation(acc[:, 64:66], xpfix[:, f, 2:4], AF.Identity, scale=fw(2), bias=fb), r=[("xpfix", f), "cvec"], w=[ak])
                        V(lambda e, acc=acc, f=f, fw=fw: e.scalar_tensor_tensor(acc[:, 64:66], xpfix[:, f, 1:3], fw(1), acc[:, 64:66], ALU.mult, ALU.add), r=[("xpfix", f), ak], w=[ak])
                        V(lambda e, acc=acc, f=f, fw=fw: e.scalar_tensor_tensor(acc[:, 64:66], xpfix[:, f, 0:2], fw(0), acc[:, 64:66], ALU.mult, ALU.add), r=[("xpfix", f), ak], w=[ak])
                        q4 = f % 4
                        tb_ = (f // 4) % 2
                        T(lambda e, xp_=xp_, q4=q4, tb_=tb_: e.transpose(tp[0:32, tb_, q4 * 128:(q4 + 1) * 128], xp_[:, 64:96], ident[:]),
                          r=[xk, "ident"], w=[f"tp{tb_}"])
                        if q4 == 3:
                            V(lambda e, tb_=tb_: e.tensor_copy(kvf[0:32, :], tp[0:32, tb_, :]), r=[f"tp{tb_}"], w=["kvf"])
                            f0 = f - 3
                            S.dma("sp", o_sfc[:, f0 * 128:f0 * 128 + 512], kvf[0:32, :], reads=["kvf"], sem="osfc")
                        if not isval:
                            A(lambda e, acc=acc: e.activation(acc, acc, AF.Gelu), r=[ak], w=[ak])
                        else:
                            fa = sp_ * 4 + j
                            G(lambda e, acc=acc, j=j, fa=fa: e.tensor_tensor(act_s[:, fa, :], gacc[:, j, 0:66], acc, ALU.mult),
                              r=[ak, ("gacc", j)], w=[("act", fa)])
            dump("s_act", act_s, [("act", f) for f in range(48)])
            S.dma("sp", xs[:], gfin_d[0:1, :].partition_broadcast(128), writes=["xs"], sem="gfin")
            for dg in range(4):
                b = next_mm(4)
                for rg in range(3):
                    i = load_slab(wslab(w_down, rg * 2048, dg * 512))
                    sk = f"slab{i}"
                    for fl in range(16):
                        f = rg * 16 + fl
                        T(lambda e, f=f, fl=fl, i=i, b=b: e.matmul(mm[0:66, b, :], lhsT=act_s[:, f, :], rhs=slabs[i][:, fl, :], start=(f == 0), stop=(f == 47)),
                          r=[sk, ("act", f)], w=[f"mm{b}"], signal=(fl == 15))
                V(lambda e, b=b, dg=dg: e.tensor_tensor(resid[0:66, 0, dg * 512:(dg + 1) * 512], mm[0:66, b, :], resid[0:66, 0, dg * 512:(dg + 1) * 512], ALU.add),
                  r=[f"mm{b}", ("resid", 0)], w=[("resid", 0)])
            A(lambda e: e.activation(gacc[0:66].rearrange("p a b -> p (a b)"), resid[0:66, 0, :], AF.Square, accum_out=stat[0:66, 8:9]),
              r=[("resid", 0)], w=[("gacc", j) for j in range(4)] + [("st8", 0)])
            A(lambda e: e.activation(stat[0:66, 12:13], stat[0:66, 8:9], AF.Sqrt, scale=1.0 / D, bias=EPS), r=[("st8", 0)], w=[("st12", 0)])
            V(lambda e: e.reciprocal(stat[0:66, 16:17], stat[0:66, 12:13]), r=[("st12", 0)], w=[("st16", 0)])
            V(lambda e: e.scalar_tensor_tensor(resid[0:66, 0, :], resid[0:66, 0, :], stat[0:66, 16:17], xs[0:66, :], ALU.mult, ALU.mult),
              r=[("resid", 0), ("st16", 0), "xs"], w=[("resid", 0)])
            S.dma("sp", ys_o[:, :], resid[0:66, 0, :], reads=[("resid", 0)], sem="yo0")

        def sample_rnn(xr_s, gg_s, xc_a, ra_a, i_a, t_a, h_a, ysq_a, xcb_a):
            for c in range(8):
                cw = lambda k, c=c: cvec[:, C_CW + k * 8 + c:C_CW + k * 8 + c + 1]
                A(lambda e, c=c, cw=cw: e.activation(xc_a[:, c, :], xr_s[:, c, 48:112], AF.Identity, scale=cw(3), bias=cvec[:, C_CB + c:C_CB + c + 1]),
                  r=["xr_s", "cvec"], w=["xc"])
                for k in range(3):
                    V(lambda e, c=c, cw=cw, k=k: e.scalar_tensor_tensor(xc_a[:, c, :], xr_s[:, c, k * 16:k * 16 + 64], cw(k), xc_a[:, c, :], ALU.mult, ALU.add),
                      r=["xr_s", "xc"], w=["xc"])
            G(lambda e: e.tensor_copy(xcb[:], rt["xc"][:]), r=["xc"], w=["xcb"])
            for c in range(8):
                T(lambda e, c=c: e.matmul(sc[:, 0, 0:64], lhsT=wbd[0][:, c, :], rhs=xcb_a[:, c, :], start=True, stop=True), r=["xcb", "wbd0"], w=["sc0"])
                T(lambda e, c=c: e.matmul(sc[:, 1, 0:64], lhsT=wbd[1][:, c, :], rhs=xcb_a[:, c, :], start=True, stop=True), r=["xcb", "wbd1"], w=["sc1"])
                A(lambda e, c=c: e.activation(ra_a[:, c, :], sc[:, 0, 0:64], AF.Sigmoid, bias=cvec[:, C_BA + c:C_BA + c + 1]), r=["sc0", "cvec"], w=["ra"])
                A(lambda e, c=c: e.activation(i_a[:, c, :], sc[:, 1, 0:64], AF.Sigmoid, bias=cvec[:, C_BX + c:C_BX + c + 1]), r=["sc1", "cvec"], w=["ib"])
                A(lambda e, c=c: e.activation(ra_a[:, c, :], ra_a[:, c, :], AF.Exp, scale=cneg[:, c:c + 1]), r=["ra", "cneg"], w=["ra"])
            ra, ib, tb, xc = rt["ra"], rt["i"], rt["t"], rt["xc"]
            V(lambda e: e.tensor_tensor(tb[:], ra[:], ra[:], ALU.mult), r=["ra"], w=["tb"])
            A(lambda e: e.activation(tb[:], tb[:], AF.Sqrt, scale=-1.0, bias=1.0), r=["tb"], w=["tb"])
            G(lambda e: e.tensor_tensor(ib[:], ib[:], xc[:], ALU.mult), r=["ib", "xc"], w=["ib"])
            V(lambda e: e.tensor_tensor(ib[:], ib[:], tb[:], ALU.mult), r=["ib", "tb"], w=["ib"])
            hs = t_a
            for t in range(4):
                prev = h_a[:, :, 0:16] if t == 0 else hs[:, :, (t - 1) * 16:t * 16]
                V(lambda e, t=t, prev=prev: e.tensor_tensor(hs[:, :, t * 16:(t + 1) * 16], ra_a[:, :, t * 16:(t + 1) * 16], prev, ALU.mult),
                  r=["ra", "hb", "tb"], w=["tb"])
                V(lambda e, t=t: e.tensor_tensor(hs[:, :, t * 16:(t + 1) * 16], hs[:, :, t * 16:(t + 1) * 16], i_a[:, :, t * 16:(t + 1) * 16], ALU.add),
                  r=["ib", "tb"], w=["tb"])
            S.dma("sp", o_srh.ap(), hs[:, :, 48:64], reads=["tb"], sem="osrh")
            G(lambda e: e.tensor_tensor(gg_s, gg_s, hs, ALU.mult), r=["gg_s", "tb"], w=["gg_s"])
            G(lambda e: e.tensor_tensor(ysq_a, gg_s, gg_s, ALU.mult), r=["gg_s"], w=["ysq"])
            for c in range(8):
                T(lambda e, c=c: e.matmul(mm[:, 3, 0:64], lhsT=ones_f[:], rhs=ysq_a[:, c, :], start=(c == 0), stop=(c == 7)),
                  r=["ysq", "ones"], w=["mm3"], signal=(c == 7))

        ident_b = sb("ident_b", [128, 128], BF16)
        V(lambda e: e.tensor_copy(ident_b[:], ident[:]), r=["ident"], w=["ident_b"])

        if dbg.get("prepass", True):
            V(lambda e: e.memset(rtot[:], 0.0), w=["rtot"])
            for blk in range(NBLK):
                prompt_block(blk, pre=True)
            V(lambda e: e.tensor_tensor(exs[:, 0:8], rtot[:], cneg[:], ALU.mult), r=["rtot", "cneg"], w=["exs"])
            A(lambda e: e.activation(exs[:, 0:8], exs[:, 0:8], AF.Exp), r=["exs"], w=["exs"])
            V(lambda e: e.tensor_copy(exs[:, 8:16], hstate[:]), r=["hstate"], w=["exs"])
            S.dma("sp", ex_in[:, :], exs[:], reads=["exs"], writes=["ex_in"], sem="ex_in")
            S.custom("pool", lambda e: e.collective_compute("AllGather", ALU.bypass, replica_groups=[list(range(NCORES))],
                                                            ins=[ex_in.ap().opt()], outs=[ex_out.ap().opt()]),
                     reads=["ex_in"], writes=["ex_out"], sem="cc1", inc=1)
            S.dma("sp", exg[:], ex_out.ap().rearrange("(r p) c -> p r c", p=128), reads=["ex_out"], writes=["exg"], sem="exg")
            V(lambda e: e.memset(hstate[:], 0.0), r=["exs"], w=["hstate"])
            for r_ in range(NCORES):
                mcol = cflag[:, 1 + r_:2 + r_]
                V(lambda e, r_=r_, mcol=mcol: e.tensor_scalar(exa[:], exg[:, r_, 0:8], -1.0, mcol, ALU.add, ALU.mult), r=["exg", "cflag"], w=["exa"])
                V(lambda e: e.tensor_scalar(exa[:], exa[:], 1.0, None, ALU.add), r=["exa"], w=["exa"])
                V(lambda e: e.tensor_tensor(hstate[:], hstate[:], exa[:], ALU.mult), r=["exa", "hstate"], w=["hstate"])
                V(lambda e, r_=r_, mcol=mcol: e.scalar_tensor_tensor(hstate[:], exg[:, r_, 8:16], mcol, hstate[:], ALU.mult, ALU.add),
                  r=["exg", "cflag", "hstate"], w=["hstate"])
        if upto != 'const':
            for blk in range(nblk):
                prompt_block(blk)

        if dbg.get("sample", True) and upto == 'all' and nblk == NBLK:
            sample_block()
        S.dma("sp", o_prh[:, :], hstate[:], reads=["hstate"], sem="o_prh")
        for c in range(8):
            T(lambda e, c=c: e.transpose(tp[0:3, c // 4, (c % 4) * 128:(c % 4 + 1) * 128], xrt[:, c, :], ident[:]),
              r=["xrt", "ident"], w=[f"tp{c // 4}"], signal=(c % 4 == 3))
        V(lambda e: e.tensor_copy(small_o[0:3, 0:1024].rearrange("p (a b) -> p a b", a=2), tp[0:3, :, :]), r=["tp0", "tp1"], w=["xs"])
        S.dma("sp", o_prc[:, :], small_o[0:3, 0:1024], reads=["xs"], sem="o_prc")
        for tk in range(2):
            T(lambda e, tk=tk: e.transpose(tp[0:96, tk, 0:128], hal[:, :, tk], ident[:]), r=[("hal", f) for f in range(96)] + ["ident"], w=[f"tp{tk}"])
            V(lambda e, tk=tk: e.tensor_copy(kvf[0:96, tk * 128:(tk + 1) * 128], tp[0:96, tk, 0:128]), r=[f"tp{tk}"], w=["kvf"])
            S.dma("sp", o_pfc[tk:tk + 1, :].rearrange("o (c p) -> (o c) p", p=128), kvf[0:96, tk * 128:(tk + 1) * 128], reads=["kvf"], sem=f"o_pfc{tk}")


        S.wait_all("sp")
        S.emit()
        build.last_sched = S
        build.sbuf_left = nc.sbuf_bytes_remaining
    return nc


def _t5_bucket_np(d):
    n = np.maximum(d, 0)
    nf = np.maximum(n, 1).astype(np.float32)
    large = 16 + (np.log(nf / 16) / math.log(128 / 16) * 16).astype(np.int32)
    large = np.minimum(large, 31)
    return np.where(n < 16, n, large)


def _feat(v):
    v = np.asarray(v, np.float32).reshape(-1, 128)
    return np.ascontiguousarray(v.T)


_NC_CACHE = {}


def kernel(x_prompt, x_sample, state_rnn_conv, state_rnn_h, cache_win_k, cache_win_v, state_ffn_conv,
           norm_mix_g, w_in, rnn_conv_w, rnn_conv_b, w_gate_a, b_gate_a, w_gate_x, b_gate_x, rnn_lambda,
           attn_sinks, rel_bias_table, gn_rnn_g, gn_attn_g, w_out, norm_ffn_g, w_up, ffn_conv_w, ffn_conv_b,
           w_down, norm_final_g):
    f32 = lambda a: np.ascontiguousarray(np.asarray(a, dtype=np.float32))
    x_prompt = f32(x_prompt)
    x_sample, state_rnn_conv, state_rnn_h = f32(x_sample), f32(state_rnn_conv), f32(state_rnn_h)
    cache_win_k, cache_win_v, state_ffn_conv = f32(cache_win_k), f32(cache_win_v), f32(state_ffn_conv)
    cv = np.zeros((128, NCV), np.float32)
    cv[:, C_GMIX:C_GMIX + 16] = _feat(norm_mix_g[0])
    cv[:, C_GFFN:C_GFFN + 16] = _feat(norm_ffn_g[0])
    cv[:, C_GRNN:C_GRNN + 8] = _feat(gn_rnn_g[0])
    cv[:, C_GATT:C_GATT + 8] = _feat(gn_attn_g[0])
    for k in range(4):
        cv[:, C_CW + 8 * k:C_CW + 8 * k + 8] = _feat(np.asarray(rnn_conv_w)[0, k])
    cv[:, C_CB:C_CB + 8] = _feat(rnn_conv_b[0])
    cv[:, C_BA:C_BA + 8] = _feat(b_gate_a[0])
    cv[:, C_BX:C_BX + 8] = _feat(b_gate_x[0])
    cv[:, C_LAM:C_LAM + 8] = _feat(rnn_lambda[0])
    for k in range(3):
        cv[:, C_FW + 96 * k:C_FW + 96 * k + 96] = _feat(np.asarray(ffn_conv_w)[0, k])
    cv[:, C_FB:C_FB + 96] = _feat(ffn_conv_b[0])
    j = np.arange(384)
    dist = 255 - j
    valid = (dist >= 0) & (dist < 128)
    oh2 = np.zeros((32, 384), np.float32)
    oh2[_t5_bucket_np(dist)[valid], j[valid]] = 1.0
    mrow = np.tile(np.where(valid, 0.0, NEG).astype(np.float32)[None, :], (16, 1))
    ident = np.eye(128, dtype=np.float32)
    common = {
        "w_in": f32(w_in[0]), "w_out": f32(w_out[0]), "w_up": f32(w_up[0]), "w_down": f32(w_down[0]),
        "cvec": cv, "gfin": f32(norm_final_g).reshape(1, D), "sinks": f32(attn_sinks).reshape(1, 16),
        "table": f32(rel_bias_table), "wga": f32(w_gate_a[0]), "wgx": f32(w_gate_x[0]),
        "ident": ident, "oh2": oh2, "mrow": mrow, "aident": np.ascontiguousarray(ident[::-1]),
    }
    in_maps = []
    for c in range(NCORES):
        s, jj = c // 4, c % 4
        xpc = np.zeros((128 + NT, D), np.float32)
        t0 = jj * NT
        xpc[128:] = x_prompt[s, t0:t0 + NT]
        if jj > 0:
            xpc[:128] = x_prompt[s, t0 - 128:t0]
        cf = np.zeros((128, 32), np.float32)
        cf[:, 0] = 1.0 if jj > 0 else 0.0
        for r in range(NCORES):
            cf[:, 1 + r] = 1.0 if (r // 4 == s and r < c) else 0.0
            cf[:, 10 + r] = 1.0 if (r // 4 == s and r == c - 1) else 0.0
        cf[:, 9] = 0.0 if jj > 0 else NEG
        m = dict(common)
        m["xp"] = xpc
        m["cflag"] = cf
        sl = slice(16 * c, 16 * c + 16)
        m["xs_in"] = np.ascontiguousarray(x_sample[sl].transpose(1, 0, 2).reshape(64, D))
        m["src_conv"] = np.ascontiguousarray(state_rnn_conv[0, sl].reshape(16, 3, 8, 128).transpose(3, 2, 1, 0).reshape(128, 8, 48))
        m["srh"] = np.ascontiguousarray(state_rnn_h[0, sl].reshape(16, 8, 128).transpose(2, 1, 0))
        kc = cache_win_k[0, sl].reshape(16, 128, 256)
        m["kcT"] = np.ascontiguousarray(kc.reshape(16, 128, 2, 128).transpose(3, 2, 0, 1))
        m["vc"] = np.ascontiguousarray(cache_win_v[0, sl].reshape(16, 128, 256).transpose(1, 0, 2))
        m["kc_o"] = np.ascontiguousarray(kc)
        m["vc_o"] = np.ascontiguousarray(cache_win_v[0, sl].reshape(16, 128, 256))
        m["sfc"] = np.ascontiguousarray(state_ffn_conv[0, sl].reshape(16, 2, 96, 128).transpose(3, 2, 1, 0).reshape(128, 96, 32))
        m["sinkr"] = np.ascontiguousarray(np.repeat(f32(attn_sinks).reshape(16), 4).reshape(64, 1))
        in_maps.append(m)
    if "nc" not in _NC_CACHE:
        _NC_CACHE["nc"] = build()
    res = run_bass_kernel_spmd(_NC_CACHE["nc"], in_maps, core_ids=list(range(NCORES)))
    R = res.results
    y_prompt = np.zeros((2, 8192, D), np.float32)
    for c in range(NCORES):
        s, jj = c // 4, c % 4
        y_prompt[s, jj * NT:(jj + 1) * NT] = R[c]["yp"]
    lastc = [3, 7]
    p_rnn_conv = np.stack([R[c]["o_prc"] for c in lastc])[None]
    p_rnn_h = np.stack([np.ascontiguousarray(R[c]["o_prh"].T).reshape(-1) for c in lastc])[None]
    p_win_k = np.stack([R[c]["o_pk"].reshape(128, 4, 64) for c in lastc])[None]
    p_win_v = np.stack([R[c]["o_pv"].reshape(128, 4, 64) for c in lastc])[None]
    p_ffn = np.stack([R[c]["o_pfc"] for c in lastc])[None]
    _NC_CACHE["last"] = R
    y_sample = np.zeros((128, 4, D), np.float32)
    s_rnn_conv = np.zeros((1, 128, 3, DR), np.float32)
    s_rnn_h = np.zeros((1, 128, DR), np.float32)
    s_win_k = np.zeros((1, 128, 128, 4, 64), np.float32)
    s_win_v = np.zeros((1, 128, 128, 4, 64), np.float32)
    s_ffn = np.zeros((1, 128, 2, 12288), np.float32)
    for c in range(NCORES):
        sl = slice(16 * c, 16 * c + 16)
        ys = R[c]["ys_o"]
        y_sample[sl] = ys[0:64].reshape(4, 16, D).transpose(1, 0, 2)
        y_prompt[c // 4, (c % 4) * NT:(c % 4) * NT + 2] = ys[64:66]
        s_rnn_conv[0, sl] = R[c]["o_src"].reshape(3, 16, DR).transpose(1, 0, 2)
        s_rnn_h[0, sl] = R[c]["o_srh"].transpose(2, 1, 0).reshape(16, DR)
        s_win_k[0, sl] = R[c]["o_sk"].reshape(16, 128, 4, 64)
        s_win_v[0, sl] = R[c]["o_sv"].reshape(16, 128, 4, 64)
        s_ffn[0, sl] = R[c]["o_sfc"].reshape(2, 16, 12288).transpose(1, 0, 2)
    return (y_prompt, y_sample, p_rnn_conv, p_rnn_h, p_win_k, p_win_v, p_ffn,
            s_rnn_conv, s_rnn_h, s_win_k, s_win_v, s_ffn)
```
